# Optimizing a Trainium2 kernel written in Bass

```python
import jax, jax.numpy as jnp
from jax import lax
import numpy as np

D_MODEL = 2048
BATCH = 4
SEQ = 2048
DEPTH = 2
DEC_BATCH = 32
DEC_SEQ = 8
PAST_LEN = 16384
PAGE_SIZE = 128

N_MEM = 256
MEM_HEADS = 4
MEM_WIDTH = D_MODEL // 4
MEM_HEAD_DIM = MEM_WIDTH // MEM_HEADS
TOKEN_WIDTH = D_MODEL - MEM_WIDTH
CONV_WIDTH = 3
WINDOW = 128
HEAD_DIM = 64
N_HEADS = TOKEN_WIDTH // HEAD_DIM
N_KV_HEADS = 4
GROUP = N_HEADS // N_KV_HEADS
KV_WIDTH = N_KV_HEADS * HEAD_DIM
D_FF = ((8 * D_MODEL + 3 * 256 - 1) // (3 * 256)) * 256
N_CONV_LAYERS = (DEPTH + 1) // 2
N_ATTN_LAYERS = DEPTH // 2
EPS = 1e-6

kernel_name = "hybrid_conv_swa_sink_memxattn_decoder_step"


def rmsnorm(x, g):
    xf = x.astype(jnp.float32)
    r = lax.rsqrt(jnp.mean(xf * xf, axis=-1, keepdims=True) + EPS)
    return (xf * r).astype(x.dtype) * g


def swiglu(h, w_gate, w_up, w_down):
    return (jax.nn.silu(h @ w_gate) * (h @ w_up)) @ w_down


def cross_attention(qm, mk, mv):
    n, t = qm.shape[:2]
    q = qm.reshape(n, t, MEM_HEADS, MEM_HEAD_DIM)
    s = jnp.einsum('nqhd,nmhd->nhqm', q, mk).astype(jnp.float32) * (MEM_HEAD_DIM ** -0.5)
    p = jax.nn.softmax(s, axis=-1).astype(mv.dtype)
    o = jnp.einsum('nhqm,nmhd->nqhd', p, mv)
    return o.reshape(n, t, MEM_WIDTH)


def band_attention(q, kk, vv, sinks, key_valid):
    n, nq = q.shape[:2]
    nk = kk.shape[1]
    qg = q.reshape(n, nq, N_KV_HEADS, GROUP, HEAD_DIM)
    s = jnp.einsum('nqkgd,njkd->nkgqj', qg, kk).astype(jnp.float32) * (HEAD_DIM ** -0.5)
    qi = jnp.arange(nq)[:, None]
    kj = jnp.arange(nk)[None, :]
    band = (kj > qi) & (kj <= qi + WINDOW)
    mask = band[None, None, None] & key_valid[:, None, None, None, :]
    s = jnp.where(mask, s, -jnp.inf)
    sink = sinks.astype(jnp.float32).reshape(N_KV_HEADS, GROUP, 1, 1)
    m = jnp.maximum(jnp.max(s, axis=-1, keepdims=True), sink)
    e = jnp.exp(s - m)
    p = e / (jnp.sum(e, axis=-1, keepdims=True) + jnp.exp(sink - m))
    o = jnp.einsum('nkgqj,njkd->nqkgd', p.astype(vv.dtype), vv)
    return o.reshape(n, nq, TOKEN_WIDTH)


def conv_mixer(h, conv_prev, mk, mv, w_in, w_conv, w_out):
    t = h.shape[1]
    z = h @ w_in
    b = z[..., :TOKEN_WIDTH]
    c = z[..., TOKEN_WIDTH:2 * TOKEN_WIDTH]
    u = z[..., 2 * TOKEN_WIDTH:3 * TOKEN_WIDTH]
    qm = z[..., 3 * TOKEN_WIDTH:]
    ext = jnp.concatenate([conv_prev, c * u], axis=1)
    conv = sum(w_conv[k] * ext[:, k:k + t] for k in range(CONV_WIDTH))
    tok = b * conv
    out = jnp.concatenate([tok, cross_attention(qm, mk, mv)], axis=-1) @ w_out
    return out, ext[:, -(CONV_WIDTH - 1):]


def attn_project(h, w_in):
    n, t = h.shape[:2]
    z = h @ w_in
    q = z[..., :TOKEN_WIDTH].reshape(n, t, N_HEADS, HEAD_DIM)
    k = z[..., TOKEN_WIDTH:TOKEN_WIDTH + KV_WIDTH].reshape(n, t, N_KV_HEADS, HEAD_DIM)
    v = z[..., TOKEN_WIDTH + KV_WIDTH:TOKEN_WIDTH + 2 * KV_WIDTH].reshape(n, t, N_KV_HEADS, HEAD_DIM)
    qm = z[..., TOKEN_WIDTH + 2 * KV_WIDTH:]
    return q, k, v, qm


def swa_prompt(q, k, v, sinks):
    b, s = q.shape[:2]
    nb = s // WINDOW
    kb = k.reshape(b, nb, WINDOW, N_KV_HEADS, HEAD_DIM)
    vb = v.reshape(b, nb, WINDOW, N_KV_HEADS, HEAD_DIM)
    kk = jnp.concatenate([jnp.concatenate([jnp.zeros_like(kb[:, :1]), kb[:, :-1]], axis=1), kb], axis=2)
    vv = jnp.concatenate([jnp.concatenate([jnp.zeros_like(vb[:, :1]), vb[:, :-1]], axis=1), vb], axis=2)
    valid = (jnp.arange(nb)[:, None] > 0) | (jnp.arange(2 * WINDOW)[None, :] >= WINDOW)
    valid = jnp.broadcast_to(valid, (b, nb, 2 * WINDOW)).reshape(b * nb, 2 * WINDOW)
    o = band_attention(q.reshape(b * nb, WINDOW, N_HEADS, HEAD_DIM),
                       kk.reshape(b * nb, 2 * WINDOW, N_KV_HEADS, HEAD_DIM),
                       vv.reshape(b * nb, 2 * WINDOW, N_KV_HEADS, HEAD_DIM), sinks, valid)
    return o.reshape(b, s, TOKEN_WIDTH)


def setup_inputs(seed: int = 0) -> dict:
    key = jax.random.key(seed)
    ks = jax.random.split(key, 24)
    f32 = jnp.float32
    nrm = lambda k, shape, scale: jax.random.normal(k, shape, f32) * scale
    gain = lambda k, shape: 1.0 + 0.01 * jax.random.normal(k, shape, f32)
    d = D_MODEL
    return {
        "x_prompt": nrm(ks[0], (BATCH, SEQ, d), 1.0),
        "x_sample": nrm(ks[1], (DEC_BATCH, DEC_SEQ, d), 1.0),
        "mem_prompt": nrm(ks[2], (BATCH, N_MEM, d), 1.0),
        "state_conv": nrm(ks[3], (N_CONV_LAYERS, DEC_BATCH, CONV_WIDTH - 1, TOKEN_WIDTH), 1.0),
        "cache_win_k": nrm(ks[4], (N_ATTN_LAYERS, DEC_BATCH, WINDOW, N_KV_HEADS, HEAD_DIM), 1.0),
        "cache_win_v": nrm(ks[5], (N_ATTN_LAYERS, DEC_BATCH, WINDOW, N_KV_HEADS, HEAD_DIM), 1.0),
        "cache_mem_k": nrm(ks[6], (DEPTH, DEC_BATCH, N_MEM, MEM_HEADS, MEM_HEAD_DIM), 1.0),
        "cache_mem_v": nrm(ks[7], (DEPTH, DEC_BATCH, N_MEM, MEM_HEADS, MEM_HEAD_DIM), 1.0),
        "norm_mix": gain(ks[8], (DEPTH, d)),
        "norm_mem": gain(ks[9], (DEPTH, d)),
        "w_mem_kv": nrm(ks[10], (DEPTH, d, 2 * MEM_WIDTH), d ** -0.5),
        "norm_ffn": gain(ks[11], (DEPTH, d)),
        "w_gate": nrm(ks[12], (DEPTH, d, D_FF), d ** -0.5),
        "w_up": nrm(ks[13], (DEPTH, d, D_FF), d ** -0.5),
        "w_down": nrm(ks[14], (DEPTH, D_FF, d), D_FF ** -0.5),
        "conv_w_in": nrm(ks[15], (N_CONV_LAYERS, d, 3 * TOKEN_WIDTH + MEM_WIDTH), d ** -0.5),
        "conv_w": nrm(ks[16], (N_CONV_LAYERS, CONV_WIDTH, TOKEN_WIDTH), CONV_WIDTH ** -0.5),
        "conv_w_out": nrm(ks[17], (N_CONV_LAYERS, TOKEN_WIDTH + MEM_WIDTH, d), (TOKEN_WIDTH + MEM_WIDTH) ** -0.5),
        "attn_w_in": nrm(ks[18], (N_ATTN_LAYERS, d, TOKEN_WIDTH + 2 * KV_WIDTH + MEM_WIDTH), d ** -0.5),
        "attn_sinks": nrm(ks[19], (N_ATTN_LAYERS, N_HEADS), 0.5),
        "attn_w_out": nrm(ks[20], (N_ATTN_LAYERS, TOKEN_WIDTH + MEM_WIDTH, d), (TOKEN_WIDTH + MEM_WIDTH) ** -0.5),
        "norm_final": gain(ks[21], (d,)),
    }


def reference(x_prompt, x_sample, mem_prompt, state_conv, cache_win_k, cache_win_v, cache_mem_k, cache_mem_v,
              norm_mix, norm_mem, w_mem_kv, norm_ffn, w_gate, w_up, w_down,
              conv_w_in, conv_w, conv_w_out, attn_w_in, attn_sinks, attn_w_out, norm_final):
    xp, xs = x_prompt, x_sample
    bp, dbs = xp.shape[0], xs.shape[0]
    conv_p, conv_s, wk_p, wv_p, wk_s, wv_s, mk_p, mv_p = [], [], [], [], [], [], [], []
    for i in range(DEPTH):
        mkv = rmsnorm(mem_prompt, norm_mem[i]) @ w_mem_kv[i]
        mk = mkv[..., :MEM_WIDTH].reshape(bp, N_MEM, MEM_HEADS, MEM_HEAD_DIM)
        mv = mkv[..., MEM_WIDTH:].reshape(bp, N_MEM, MEM_HEADS, MEM_HEAD_DIM)
        mk_p.append(mk)
        mv_p.append(mv)
        hp = rmsnorm(xp, norm_mix[i])
        hs = rmsnorm(xs, norm_mix[i])
        j = i // 2
        if i % 2 == 0:
            zero_prev = jnp.zeros((bp, CONV_WIDTH - 1, TOKEN_WIDTH), hp.dtype)
            op, st_p = conv_mixer(hp, zero_prev, mk, mv, conv_w_in[j], conv_w[j], conv_w_out[j])
            os_, st_s = conv_mixer(hs, state_conv[j], cache_mem_k[i], cache_mem_v[i],
                                   conv_w_in[j], conv_w[j], conv_w_out[j])
            conv_p.append(st_p)
            conv_s.append(st_s)
        else:
            q, k, v, qm = attn_project(hp, attn_w_in[j])
            a = swa_prompt(q, k, v, attn_sinks[j])
            op = jnp.concatenate([a, cross_attention(qm, mk, mv)], axis=-1) @ attn_w_out[j]
            wk_p.append(k[:, -WINDOW:])
            wv_p.append(v[:, -WINDOW:])
            q, k, v, qm = attn_project(hs, attn_w_in[j])
            kk = jnp.concatenate([cache_win_k[j], k], axis=1)
            vv = jnp.concatenate([cache_win_v[j], v], axis=1)
            valid = jnp.ones((dbs, kk.shape[1]), dtype=bool)
            a = band_attention(q, kk, vv, attn_sinks[j], valid)
            os_ = jnp.concatenate([a, cross_attention(qm, cache_mem_k[i], cache_mem_v[i])], axis=-1) @ attn_w_out[j]
            wk_s.append(kk[:, -WINDOW:])
            wv_s.append(vv[:, -WINDOW:])
        xp = xp + op
        xs = xs + os_
        xp = xp + swiglu(rmsnorm(xp, norm_ffn[i]), w_gate[i], w_up[i], w_down[i])
        xs = xs + swiglu(rmsnorm(xs, norm_ffn[i]), w_gate[i], w_up[i], w_down[i])
    y_prompt = rmsnorm(xp, norm_final)
    y_sample = rmsnorm(xs, norm_final)
    new_conv_prompt = jnp.stack(conv_p)
    new_conv_sample = jnp.stack(conv_s)
    new_win_k_prompt = jnp.stack(wk_p)
    new_win_v_prompt = jnp.stack(wv_p)
    new_win_k_sample = jnp.stack(wk_s)
    new_win_v_sample = jnp.stack(wv_s)
    new_mem_k_prompt = jnp.stack(mk_p)
    new_mem_v_prompt = jnp.stack(mv_p)
    return (y_prompt, y_sample, new_conv_prompt, new_conv_sample, new_win_k_prompt, new_win_v_prompt,
            new_win_k_sample, new_win_v_sample, new_mem_k_prompt, new_mem_v_prompt)
```

```python
import os
import numpy as np
from contextlib import ExitStack
import concourse.bass as bass
import concourse.mybir as mybir
from concourse.bass_utils import run_bass_kernel_spmd

F32 = mybir.dt.float32
BF16 = mybir.dt.bfloat16
AF = mybir.ActivationFunctionType
ALU = mybir.AluOpType
AX = mybir.AxisListType
ENGS = ("pe", "act", "dve", "pool", "sp")

D = 2048
FF = 5632
TOK = 1186
C_HALO, C_MAIN, C_SAMP = 2, 130, 1154
NS = 4
EPS = 1e-6
NEG = -30000.0


class Prog:
    def __init__(self, nc):
        self.nc = nc
        self.ops = []
        self.last_w = {}
        self.readers = {}
        self.es = ExitStack()
        self.dma_keys = []

    def sb(self, name, shape, dt):
        return self.es.enter_context(self.nc.sbuf_tensor("sb_" + name, list(shape), dt))

    def ps(self, name, shape, dt=F32):
        return self.es.enter_context(self.nc.psum_tensor(name, list(shape), dt))

    def op(self, eng, fn, r=(), w=(), dma=None, ndma=1):
        w = list(w) + [x for x in r if x.startswith("ps")]
        r = [x for x in r if not x.startswith("ps")]
        i = len(self.ops)
        deps = set()
        for x in r:
            j = self.last_w.get(x)
            if j is not None:
                deps.add(j)
        for x in w:
            j = self.last_w.get(x)
            if j is not None:
                deps.add(j)
            deps.update(self.readers.get(x, ()))
        for x in r:
            self.readers.setdefault(x, []).append(i)
        for x in w:
            self.last_w[x] = i
            self.readers[x] = []
        if dma is not None and dma not in self.dma_keys:
            self.dma_keys.append(dma)
        self.ops.append(dict(eng=eng, fn=fn, deps=deps, dma=dma, ndma=ndma, sig=False))
        return i

    def emit(self):
        nc = self.nc
        ops = self.ops
        for o in ops:
            nd = set()
            for j in o["deps"]:
                p = ops[j]
                if p["dma"] is None and o["dma"] is None and p["eng"] == "pe" and o["eng"] == "pe":
                    continue
                nd.add(j)
                p["sig"] = True
            o["deps"] = nd
        cnt = {e: 0 for e in ENGS}
        dcnt = {k: 0 for k in self.dma_keys}
        for o in ops:
            if o["dma"] is not None:
                dcnt[o["dma"]] += 16 * o["ndma"]
                o["sv"] = dcnt[o["dma"]]
            elif o["sig"]:
                cnt[o["eng"]] += 1
                o["sv"] = cnt[o["eng"]]
        esem = {e: self.es.enter_context(nc.semaphore("s_" + e)) for e in ENGS}
        dsem = {k: self.es.enter_context(nc.semaphore("d_%d" % n)) for n, k in enumerate(self.dma_keys)}

        def run(eng_name, eh):
            waited = {}
            for o in ops:
                if o["eng"] != eng_name:
                    continue
                need = {}
                for j in o["deps"]:
                    p = ops[j]
                    key = ("d", p["dma"]) if p["dma"] is not None else ("e", p["eng"])
                    if p["sv"] > need.get(key, 0):
                        need[key] = p["sv"]
                for key, v in need.items():
                    if waited.get(key, 0) >= v:
                        continue
                    waited[key] = v
                    sem = dsem[key[1]] if key[0] == "d" else esem[key[1]]
                    eh.wait_ge(sem, v)
                if o["fn"] is None:
                    continue
                ins = o["fn"](eh)
                if o["dma"] is not None:
                    if not isinstance(ins, (list, tuple)):
                        ins = [ins]
                    assert len(ins) == o["ndma"], (len(ins), o["ndma"])
                    for x in ins:
                        x.then_inc(dsem[o["dma"]], 16)
                elif o["sig"]:
                    ins.then_inc(esem[eng_name], 1)

        with nc.Block() as block:
            @block.tensor
            def _(e):
                run("pe", e)

            @block.scalar
            def _(e):
                run("act", e)

            @block.vector
            def _(e):
                run("dve", e)

            @block.gpsimd
            def _(e):
                run("pool", e)

            @block.sync
            def _(e):
                run("sp", e)
        self.es.close()


class Reg:
    def __init__(self, P, name, nbytes, gran=512):
        self.name = name
        self.gran = gran
        self.nbytes = nbytes
        self.t = P.sb(name, [128, nbytes // 2], BF16)


class View:
    def __init__(self, reg, off, shape, dt):
        self.reg, self.off, self.shape, self.dt = reg, off, tuple(shape), dt
        self.esz = 2 if dt == BF16 else 4
        n = int(np.prod(shape))
        assert off % 4 == 0 and off + n * self.esz <= reg.nbytes, (reg.name, off, shape)
        raw = reg.t[:, off // 2:(off + n * self.esz) // 2]
        if dt == F32:
            raw = raw.bitcast(F32)
        if len(shape) == 2:
            raw = raw.rearrange("p (a b) -> p a b", a=shape[0])
        elif len(shape) == 3:
            raw = raw.rearrange("p (a b c) -> p a b c", a=shape[0], b=shape[1])
        elif len(shape) == 4:
            raw = raw.rearrange("p (a b c d) -> p a b c d", a=shape[0], b=shape[1], c=shape[2])
        self.ap = raw
        self.strides = [int(np.prod(shape[i + 1:])) * self.esz for i in range(len(shape))]

    def r(self, *idx):
        idx = list(idx) + [None] * (len(self.shape) - len(idx))
        rngs = []
        for d, ix in enumerate(idx):
            if ix is None:
                rngs.append((0, self.shape[d]))
            elif isinstance(ix, tuple):
                rngs.append(ix)
            else:
                rngs.append((ix, ix + 1))
        names = set()
        g = self.reg.gran

        def rec(d, base):
            a, b = rngs[d]
            full_after = all(rngs[k] == (0, self.shape[k]) for k in range(d + 1, len(rngs)))
            if full_after or d == len(rngs) - 1:
                lo = base + a * self.strides[d]
                hi = base + b * self.strides[d]
                for i in range(lo // g, (hi - 1) // g + 1):
                    names.add("%s%d" % (self.reg.name, i))
            else:
                for i in range(a, b):
                    rec(d + 1, base + i * self.strides[d])

        rec(0, self.off)
        return sorted(names)


def build():
    nc = bass.Bass("TRN2", target_bir_lowering=False)

    def din(name, shape, dt=F32):
        return nc.dram_tensor(name, list(shape), dt, kind="ExternalInput").ap()

    def dout(name, shape):
        return nc.dram_tensor(name, list(shape), F32, kind="ExternalOutput").ap()

    xe = din("xe", [TOK, D])
    mem = din("mem", [256, D])
    sconv = din("sconv", [8, 1536])
    cwk = din("cwk", [4, 128, 256])
    cwv = din("cwv", [4, 128, 256])
    cmk = din("cmk", [2, 4, 256, 512])
    cmv = din("cmv", [2, 4, 256, 512])
    wmem = din("w_mem_kv", [2, D, 1024])
    wg = din("w_gate", [2, D, FF])
    wu = din("w_up", [2, D, FF])
    wd = din("w_down", [2, FF, D])
    cwin = din("conv_w_in", [D, 5120])
    cwout = din("conv_w_out", [D, D])
    awin = din("attn_w_in", [D, 2560])
    awout = din("attn_w_out", [D, D])
    gvec_d = din("gvec", [128, 7 * 16])
    convw_d = din("convw", [128, 36])
    sinks_d = din("sinks", [128, 24])
    ident_d = din("ident", [128, 128])
    mask_d = din("mask", [128, 768])
    sinks24_d = din("sinks24", [128, 8])
    y_d = dout("y", [1056, D])
    oconv_d = dout("o_conv", [10, 1536])
    owkvp_d = dout("o_wkv_p", [128, 512])
    owks_d = dout("o_wk_s", [4, 128, 256])
    owvs_d = dout("o_wv_s", [4, 128, 256])
    omkv_d = dout("o_mkv", [2, 256, 1024])

    P = Prog(nc)
    RX = Reg(P, "rx", 16 * TOK * 4, gran=1024)
    xT = View(RX, 0, (16, TOK), F32)
    RA = Reg(P, "ra", 16 * TOK * 2)
    mixT = View(RA, 0, (16, TOK), BF16)
    xs = [View(RA, 8192 * i, (D,), F32) for i in range(2)]
    memT = View(RA, 16384, (16, 256), F32)
    stg = [View(RA, 32768 + 1024 * i, (256,), F32) for i in range(2)]
    yT = View(RA, 0, (16, 128), F32)
    mstg = View(RA, 34816, (3, 256), F32)
    ostg = [View(RA, 8192 + 8192 * i, (D,), F32) for i in range(2)]
    RB = Reg(P, "rb", 20480)
    hTt = View(RB, 0, (16, 512), BF16)
    hmemT = View(RB, 0, (16, 256), BF16)
    xsq = [View(RB, 16384 + 1024 * i, (512,), BF16) for i in range(2)]
    rr = View(RB, 18432, (512,), F32)
    rr_default = rr
    rr3 = [View(RB, 2048 * i, (512,), F32) for i in range(3)]
    rrf = [View(RA, 16384 + 2048 * i, (512,), F32) for i in range(3)]
    smk = [View(RB, 2048 * i, (4, 256), BF16) for i in range(4)]
    smv = [View(RB, 8192 + 2048 * i, (2, 512), BF16) for i in range(4)]
    cmkst = View(RB, 16384, (2, 512), BF16)
    actr = [View(RB, 9488 * i, (4, TOK), BF16) for i in range(2)]
    Qs = View(RB, 0, (16, 24), BF16)
    ocstg = View(RB, 0, (1536,), F32)
    RC = Reg(P, "rc", 37120)
    csb = [View(RC, 2048 * i, (512,), F32) for i in range(2)]
    cub = [View(RC, 4096 + 2064 * i, (516,), F32) for i in range(2)]
    yb = [View(RC, 8224 + 2048 * i, (512,), F32) for i in range(2)]
    cus = [View(RC, 12320 + 160 * i, (4, 10), F32) for i in range(2)]
    ysb = [View(RC, 12640 + 128 * i, (4, 8), F32) for i in range(2)]
    KT = View(RC, 0, (4, 1152), BF16)
    ksts = [View(RC, 13568 + 1024 * i, (4, 128), BF16) for i in range(4)]
    sKT = View(RC, 9216, (4, 4, 136), BF16)
    Vd = View(RC, 13568, (9, 512), BF16)
    sVc = View(RC, 22784, (4, 512), BF16)
    sVn = View(RC, 26880, (4, 512), BF16)
    kvstg = View(RC, 30976, (512,), F32)
    mkT = View(RC, 33024, (4, 256), BF16)
    mv = View(RC, 35072, (2, 512), BF16)
    mkT1p = View(RC, 16384, (4, 256), BF16)
    mv1p = View(RC, 18432, (2, 512), BF16)
    sg = [View(RC, 2048 * i, (512,), F32) for i in range(2)]
    RS = [Reg(P, "slot%d" % i, 8192, gran=8192) for i in range(NS)]
    s_in = [View(RS[i], 0, (16, 256), BF16) for i in range(NS)]
    s_row = [View(RS[i], 0, (2, D), BF16) for i in range(NS)]
    s_kd = [View(RS[i], 0, (16, 2, 2, 64), BF16) for i in range(NS)]
    s_r4 = [View(RS[i], 0, (4, 1024), BF16) for i in range(NS)]
    Er = [View(Reg(P, "er%d" % i, 528, gran=528), 0, (264,), BF16) for i in range(2)]
    Pr = [View(Reg(P, "pr%d" % i, 512, gran=512), 0, (256,), BF16) for i in range(2)]
    PTr = [View(Reg(P, "ptr%d" % i, 512, gran=512), 0, (2, 128), BF16) for i in range(2)]
    ident = P.sb("ident", [128, 128], F32)
    identb = P.sb("identb", [128, 128], BF16)
    onesb = P.sb("onesb", [128, 128], BF16)
    maskb = P.sb("maskb", [128, 3, 256], BF16)
    onesf = P.sb("onesf", [1, 128], F32)
    swp = P.sb("swp", [128, 128], BF16)
    snk = P.sb("snk", [128, 2, 24], BF16)
    snk24 = P.sb("snk24", [128, 2, 8], BF16)
    s24f = P.sb("s24f", [128, 2, 8], F32)
    snkf = P.sb("snkf", [128, 24], F32)
    gvec = P.sb("gvec", [128, 7, 16], F32)
    convw = P.sb("convw", [128, 12, 3], F32)
    sinks = P.sb("sinks", [128, 24], F32)
    carry = P.sb("carry", [128, 12, 2], F32)
    oconvT = P.sb("oconvT", [128, 12, 10], F32)
    sT = P.sb("sT", [128, 12, 8], F32)
    st = [P.sb("st%d" % i, [128, 8], F32) for i in range(6)]
    banks = [P.ps("bank%d" % i, [128, 512], F32) for i in range(8)]
    POOLS = {"G": list(range(8)), "S": list(range(8)), "PT": list(range(8)), "O": list(range(8)), "KV": [3, 4, 5, 6, 7]}
    bk = {"G": 0, "S": 0, "PT": 0, "O": 0, "KV": 0}

    def set_pools(g, s_, pt, o):
        def comp(_):
            POOLS["G"], POOLS["S"], POOLS["PT"], POOLS["O"] = g, s_, pt, o
        item(comp)

    def bank(pool="G"):
        if pool != "KV" and POOLS[pool] == POOLS["G"]:
            pool = "G"
        lst = POOLS[pool]
        i = lst[bk[pool] % len(lst)]
        bk[pool] += 1
        return banks[i], "ps%d" % i

    items = []
    outs = []
    BARRIER = object()

    def item(comp, ld=None):
        items.append((ld, comp))

    def MM(out, lhsT, rhs, start, stop, r, w):
        P.op("pe", lambda e: e.matmul(out, lhsT=lhsT, rhs=rhs, start=start, stop=stop), r=r, w=w)

    def TRN(out, in_, idn, r, w):
        P.op("pe", lambda e: e.transpose(out=out, in_=in_, identity=idn), r=r, w=w)

    def ACTF(out, in_, func, r, w, **kw):
        P.op("act", lambda e: e.activation(out=out, in_=in_, func=func, **kw), r=r, w=w)

    def TT(eng, out, in0, in1, op, r, w):
        P.op(eng, lambda e: e.tensor_tensor(out=out, in0=in0, in1=in1, op=op), r=r, w=w)

    def TS(eng, out, in0, s1, op0, r, w):
        P.op(eng, lambda e: e.tensor_scalar(out=out, in0=in0, scalar1=s1, scalar2=None, op0=op0), r=r, w=w)

    def STT(out, in0, scalar, in1, op0, op1, r, w):
        P.op("dve", lambda e: e.scalar_tensor_tensor(out=out, in0=in0, scalar=scalar, in1=in1, op0=op0, op1=op1), r=r, w=w)

    pq = [0]
    PQD = int(os.environ.get("PQD", "8"))

    def DMA(eng, pairs, r, w, key):
        if eng == "pool":
            w = list(w) + ["pq%d" % (pq[0] % PQD)]
            pq[0] += 1
        P.op(eng, lambda e: [e.dma_start(out=o, in_=i) for o, i in pairs], r=r, w=w, dma=key, ndma=len(pairs))

    def cp(eng, out, in_, r, w):
        if eng == "act":
            ACTF(out, in_, AF.Copy, r, w)
        else:
            P.op(eng, lambda e: e.tensor_copy(out=out, in_=in_), r=r, w=w)

    feed = {"steps": [], "every": 0, "cnt": 0}

    def tick():
        if feed["every"] and feed["steps"]:
            feed["cnt"] += 1
            if feed["cnt"] % feed["every"] == 0:
                feed["steps"].pop(0)(None)

    def start_feed(steps, every):
        def comp(_):
            feed["steps"], feed["every"], feed["cnt"] = list(steps), every, 0
        item(comp)

    def flush_feed():
        def comp(_):
            while feed["steps"]:
                feed["steps"].pop(0)(None)
            feed["every"] = 0
        item(comp)

    def mm16(out_ap, br, lhs, rhs, rres):
        for kc in range(16):
            MM(out_ap, lhs(kc), rhs(kc), kc == 0, kc == 15, rres(kc), [br])
            tick()

    def consts(_):
        for name, t, src in (("ident", ident[:], ident_d[:, :]),
                             ("gvec", gvec[:], gvec_d.rearrange("p (a b) -> p a b", a=7)),
                             ("convw", convw[:], convw_d.rearrange("p (a b) -> p a b", a=12)), ("sinks", sinks[:], sinks_d[:, :])):
            DMA("sp", [(t, src)], [], [name], name)
        ACTF(identb[:], ident[:], AF.Copy, ["ident"], ["identb"])
        DMA("sp", [(mstg.ap, mask_d.rearrange("p (a b) -> p a b", a=3))], [], mstg.r(), "mstg")
        ACTF(maskb[:], mstg.ap, AF.Copy, mstg.r(), ["maskb"])
        P.op("dve", lambda e: e.memset(onesf[:], 1.0), w=["onesf"])
        P.op("dve", lambda e: e.memset(swp[:], 0.0), w=["swp"])
        cp("dve", snk[:, 0, :], sinks[:], ["sinks"], ["snk"])
        TT("dve", snkf[:], sinks[:], snk[:, 0, :], ALU.subtract, ["sinks", "snk"], ["snkf"])
        cp("dve", snk[:, 1, :], snkf[:], ["snkf"], ["snk"])
        DMA("sp", [(s24f[:, 0, :], sinks24_d[:, :])], [], ["s24f"], "s24f")
        cp("dve", snk24[:, 0, :], s24f[:, 0, :], ["s24f"], ["snk24"])
        TT("dve", s24f[:, 1, :], s24f[:, 0, :], snk24[:, 0, :], ALU.subtract, ["s24f", "snk24"], ["s24f"])
        cp("dve", snk24[:, 1, :], s24f[:, 1, :], ["s24f"], ["snk24"])
        cp("dve", swp[0:64, 64:128], identb[0:64, 0:64], ["identb"], ["swp"])
        cp("dve", swp[64:128, 0:64], identb[64:128, 64:128], ["identb"], ["swp"])
        P.op("dve", lambda e: e.memset(onesb[:], 1.0), w=["onesb"])
        P.op("dve", lambda e: e.memset(carry[:], 0.0), w=["carry%d" % i for i in range(12)])
        P.op("dve", lambda e: e.memset(oconvT[:], 0.0), w=["oconvT"])

    item(consts)

    ldT_n = [0]

    def load_T(rows, n, dst4, dst4_res):
        def comp(_):
            b = ldT_n[0] % 2
            ldT_n[0] += 1
            X = xs[b]
            DMA("sp", [(X.ap[:n, :], rows)], [], X.r(), "xs%d" % b)
            for q in range(4):
                bt, br = bank()
                for j in range(4):
                    c = 4 * q + j
                    TRN(bt[:, j * 128:j * 128 + n], X.ap[:n, c * 128:(c + 1) * 128], ident[:n, :n], X.r() + ["ident"], [br])
                src = bt[:].rearrange("p (a b) -> p a b", a=4)[:, :, :n]
                cp("act" if q % 2 == 0 else "dve", dst4(q), src, [br], dst4_res(q))
        item(comp)

    def norm_tile(src, src_res, n, gi, dst, dst_res, rrv=None):
        def comp(_, rr=None):
            rr = rrv if rrv is not None else rr_default
            bt, br = bank()
            for c in range(16):
                q = xsq[c % 2]
                ACTF(q.ap[:, :n], src(c), AF.Square, src_res(c), q.r())
                MM(bt[:, :n], onesb[:], q.ap[:, :n], c == 0, c == 15, q.r() + ["onesb"], [br])
            ACTF(rr.ap[:, :n], bt[:, :n], AF.Sqrt, [br], rr.r(), scale=1.0 / D, bias=EPS)
            P.op("dve", lambda e: e.reciprocal(out=rr.ap[:, :n], in_=rr.ap[:, :n]), r=rr.r(), w=rr.r())
            for c in range(16):
                STT(dst(c), src(c), gvec[:, gi, c:c + 1], rr.ap[:, :n], ALU.mult, ALU.mult, src_res(c) + rr.r() + ["gvec"], dst_res(c))
        item(comp)

    def win(W, c0, ncols=256):
        src = W.rearrange("(kc p) n -> p kc n", p=128)[:, :, c0:c0 + ncols]

        def ld(s):
            DMA("pool", [(s_in[s].ap[:, :, 0:ncols], src)], [], s_in[s].r(), "slot%d" % s)
        return ld

    def win2(Wa, ca, Wb, cb):
        sa = Wa.rearrange("(kc p) n -> p kc n", p=128)[:, :, ca:ca + 128]
        sb_ = Wb.rearrange("(kc p) n -> p kc n", p=128)[:, :, cb:cb + 128]

        def ld(s):
            DMA("pool", [(s_in[s].ap[:, :, 0:128], sa), (s_in[s].ap[:, :, 128:256], sb_)], [], s_in[s].r(), "slot%d" % s)
        return ld

    def wrow4(W, r0, c0):
        src = W.rearrange("(rc p) n -> p rc n", p=128)[:, r0:r0 + 4, c0:c0 + 1024]

        def ld(s):
            DMA("pool", [(s_r4[s].ap, src)], [], s_r4[s].r(), "slot%d" % s)
        return ld

    au = [0]
    NST = 8

    def attn_unit(nq, q_ap, q_res, kt_ap, kt_res, nk, mask_ap, sink_ap, vch, out_ap, out_res, p0, p1, xsets=False, pre=None, snk_t=None, out_src=None):
        d = {}
        ne = nk + 1 if mask_ap is not None else nk

        def s0():
            if pre is not None:
                pre()
            u = au[0]
            au[0] += 1
            d["u"] = u
            d.update(E=Er[u % 2], Pm=Pr[u % 2], PT=PTr[u % 2])
            d.update(stt=st[u % 6], sres=["st%d" % (u % 6)])
            bt, br = bank("S")
            if mask_ap is not None:
                MM(bt[:nq, :nk], q_ap, kt_ap, True, False, q_res + kt_res, [br])
                MM(bt[:nq, :nk], identb[:nq, :nq], mask_ap, False, True, ["identb", "maskb"], [br])
                sk = snk if snk_t is None else snk_t
                MM(bt[:nq, nk:ne], identb[:nq, :nq], sk[:nq, 0, sink_ap:sink_ap + 1], True, False, ["identb", "snk", "snk24"], [br])
                MM(bt[:nq, nk:ne], identb[:nq, :nq], sk[:nq, 1, sink_ap:sink_ap + 1], False, True, ["identb", "snk", "snk24"], [br])
            else:
                MM(bt[:nq, :nk], q_ap, kt_ap, True, True, q_res + kt_res, [br])
            d.update(bt=bt, br=br)

        def s1():
            stt, sres, bt, br = d["stt"], d["sres"], d["bt"], d["br"]
            sin, sres_in = bt[:nq, :ne], [br]
            P.op("dve", lambda e: e.tensor_reduce(out=stt[:nq, 1:2], in_=sin, axis=AX.X, op=ALU.max, negate=True), r=sres_in, w=sres)
            d.update(sin=sin, sres_in=sres_in)

        def s2():
            E, stt, sres, sin, sres_in = d["E"], d["stt"], d["sres"], d["sin"], d["sres_in"]
            ACTF(E.ap[:nq, :ne], sin, AF.Exp, sres_in + sres, E.r() + sres, bias=stt[:nq, 1:2], scale=1.0, accum_out=stt[:nq, 2:3])

        def s3():
            E, Pm, stt, sres = d["E"], d["Pm"], d["stt"], d["sres"]
            P.op("dve", lambda e: e.reciprocal(out=stt[:nq, 5:6], in_=stt[:nq, 2:3]), r=sres, w=sres)
            TS("dve", Pm.ap[:nq, :nk], E.ap[:nq, :nk], stt[:nq, 5:6], ALU.mult, E.r() + sres, Pm.r())

        def s4():
            Pm = d["Pm"]
            lst = POOLS["PT"]
            bi_ = lst[d["u"] % len(lst)]
            bt2, br2 = banks[bi_], "ps%d" % bi_
            b2 = bt2[:].bitcast(BF16)
            for j, (v_ap, v_res, koff, nkc) in enumerate(vch):
                TRN(b2[:nkc, j * 128:j * 128 + nq], Pm.ap[:nq, koff:koff + nkc], identb[:nq, :nq], Pm.r() + ["identb"], [br2])
            d.update(b2=b2, br2=br2)

        def s5():
            PT, b2, br2 = d["PT"], d["b2"], d["br2"]
            eng = "act"
            if all(v[3] == 128 for v in vch):
                cp(eng, PT.ap[:, :, :nq], b2[:, 0:256].rearrange("p (j q) -> p j q", j=2)[:, :, :nq], [br2], PT.r())
            else:
                for j, (v_ap, v_res, koff, nkc) in enumerate(vch):
                    cp(eng, PT.ap[:nkc, j, :nq], b2[:nkc, j * 128:j * 128 + nq], [br2], PT.r())

        def s6():
            PT = d["PT"]
            lst = POOLS["PT"]
            bi_ = lst[d["u"] % len(lst)]
            bt3, br3 = banks[bi_], "ps%d" % bi_
            for j, (v_ap, v_res, koff, nkc) in enumerate(vch):
                MM(bt3[:, 128:128 + nq], v_ap, PT.ap[:nkc, j, :nq], j == 0, j == len(vch) - 1, PT.r() + v_res, [br3])
            d.update(bt3=bt3, br3=br3)

        def s7():
            src = d["bt3"][p0:p1, 128:128 + nq]
            cp("dve", out_ap, src if out_src is None else out_src(src), [d["br3"]], out_res)
        return [s0, s1, s2, s3, s4, s5, s6, s7]

    def attn_steps(units):
        n = len(units)
        steps = []
        for t in range(n + NST - 1):
            def comp(_, t=t):
                for sidx in range(NST - 1, -1, -1):
                    k = t - sidx
                    if 0 <= k < n:
                        units[k][sidx]()
            steps.append(comp)
        return steps

    def attn_run(units):
        for c in attn_steps(units):
            item(c)

    def collect(fn):
        n0 = len(items)
        fn()
        out = items[n0:]
        del items[n0:]
        return out

    def interleave(A, steps):
        na, nb = len(A), len(steps)
        j = 0
        for i, it in enumerate(A):
            items.append(it)
            tgt = (i + 1) * nb // na
            while j < tgt:
                items.append(steps[j]) if isinstance(steps[j], tuple) else item(steps[j])
                j += 1
        while j < nb:
            items.append(steps[j]) if isinstance(steps[j], tuple) else item(steps[j])
            j += 1

    T0 = [(0, 512), (512, 1024), (1024, TOK)]
    T1 = [(2, 514), (514, 1026), (1026, TOK)]
    T1m = [(130, 642), (642, 1154), (1154, TOK)]

    def phase_a():
        for rt in range(10):
            r0 = rt * 128
            n = min(128, TOK - r0)
            load_T(xe[r0:r0 + n, :], n, lambda q, r0=r0, n=n: xT.ap[:, 4 * q:4 * q + 4, r0:r0 + n],
                   lambda q, r0=r0, n=n: xT.r((4 * q, 4 * q + 4), (r0, r0 + n)))

    def mem_kv(l, mkT, mv):
        for mt in range(2):
            load_T(mem[mt * 128:(mt + 1) * 128, :], 128, lambda q, mt=mt: memT.ap[:, 4 * q:4 * q + 4, mt * 128:(mt + 1) * 128],
                   lambda q, mt=mt: memT.r((4 * q, 4 * q + 4), (mt * 128, (mt + 1) * 128)))
        norm_tile(lambda c: memT.ap[:, c, :], lambda c: memT.r(c), 256, 5 + l, lambda c: hmemT.ap[:, c, :], lambda c: hmemT.r(c))
        sn = [0]
        for blk in range(4):
            def comp(s, blk=blk):
                W = s_in[s]
                if blk < 2:
                    for hh in range(2):
                        h = 2 * blk + hh
                        bt, br = bank()
                        mm16(bt[:, 0:256], br, lambda kc: W.ap[:, kc, hh * 128:(hh + 1) * 128], lambda kc: hmemT.ap[:, kc, :],
                             lambda kc: W.r() + hmemT.r(kc))
                        cp("act", mkT.ap[:, h, :], bt[:, 0:256], [br], mkT.r(h))
                for mt in range(2):
                    bt, br = bank()
                    mm16(bt[:, 0:256], br, lambda kc: hmemT.ap[:, kc, mt * 128:(mt + 1) * 128], lambda kc: W.ap[:, kc, :],
                         lambda kc: W.r() + hmemT.r(kc))
                    k = sn[0] % 2
                    sn[0] += 1
                    sb_ = stg[k]
                    cp("dve", sb_.ap, bt[:, 0:256], [br], sb_.r())
                    if blk >= 2:
                        cp("act", mv.ap[:, mt, (blk - 2) * 256:(blk - 1) * 256], bt[:, 0:256], [br], mv.r(mt))
                    on_ = "omkv%d_%d_%d" % (l, blk, mt)
                    outs.append(on_)
                    DMA("sp", [(omkv_d[l, mt * 128:(mt + 1) * 128, blk * 256:(blk + 1) * 256], sb_.ap)], sb_.r(), [on_], "stg%d" % k)
            item(comp, win(wmem[l], blk * 256))

    def xattn(l, blocks, with_sample=True):
        units = []
        for (c0, nq) in blocks:
            for h in range(4):
                vch = [(mv.ap[:, mt, h * 128:(h + 1) * 128], mv.r(mt), mt * 128, 128) for mt in range(2)]
                units.append(attn_unit(nq, mixT.ap[:, 12 + h, c0:c0 + nq], mixT.r(12 + h, (c0, c0 + nq)), mkT.ap[:, h, :], mkT.r(h), 256, None, None, vch,
                                       mixT.ap[:, 12 + h, c0:c0 + nq], mixT.r(12 + h, (c0, c0 + nq)), 0, 128, xsets=True))
        for s in (range(4) if with_sample else []):
            b = s

            def prep(s=s, b=b):
                DMA("pool", [(cmkst.ap, cmk[l, s].rearrange("(mt p) f -> p mt f", p=128))], [], cmkst.r(), "cmkst")
                DMA("pool", [(smv[b].ap, cmv[l, s].rearrange("(mt p) f -> p mt f", p=128))], [], smv[b].r(), "smv%d" % b)
                for h in range(4):
                    bt, br = bank()
                    b2 = bt[:].bitcast(BF16)
                    for mt in range(2):
                        TRN(b2[:, mt * 128:(mt + 1) * 128], cmkst.ap[:, mt, h * 128:(h + 1) * 128], identb[:], cmkst.r() + ["identb"], [br])
                    cp("dve" if h % 2 else "act", smk[b].ap[:, h, :], b2[:, 0:256], [br], smk[b].r(h))
            c0 = C_SAMP + 8 * s
            for h in range(4):
                vch = [(smv[b].ap[:, mt, h * 128:(h + 1) * 128], smv[b].r(mt), mt * 128, 128) for mt in range(2)]
                units.append(attn_unit(8, mixT.ap[:, 12 + h, c0:c0 + 8], mixT.r(12 + h, (c0, c0 + 8)), smk[b].ap[:, h, :], smk[b].r(h), 256, None, None, vch,
                                       mixT.ap[:, 12 + h, c0:c0 + 8], mixT.r(12 + h, (c0, c0 + 8)), 0, 128, xsets=True, pre=prep if h == 0 else None))
        return units

    def out_proj(W, TT_=None):
        TL = TT_ or T0
        for db in range(8):
            def comp(s, db=db):
                Ws = s_in[s]
                for dd in range(2):
                    d = 2 * db + dd
                    for (a, b) in TL:
                        bt, br = bank()
                        mm16(bt[:, :b - a], br, lambda kc: Ws.ap[:, kc, dd * 128:(dd + 1) * 128], lambda kc: mixT.ap[:, kc, a:b],
                             lambda kc: Ws.r() + mixT.r(kc, (a, b)))
                        TT("dve", xT.ap[:, d, a:b], xT.ap[:, d, a:b], bt[:, :b - a], ALU.add, xT.r(d, (a, b)) + [br], xT.r(d, (a, b)))
            item(comp, win(W, db * 256))

    def cache_prep(_):
        sv5 = sVc.ap.rearrange("p s (g d c) -> p s g d c", g=4, d=2)
        for dd in range(2):
            DMA("pool", [(sv5[:, s, :, dd, :], cwv[s].rearrange("p (g c) -> p g c", g=4)) for s in range(4)], [], sVc.r(), "sVc%d" % dd)
        for s in range(4):
            kst = ksts[s]
            ks4 = kst.ap.rearrange("p g (d c) -> p g d c", d=2)
            DMA("pool", [(ks4[:, :, dd, :], cwk[s].rearrange("p (g c) -> p g c", g=4)) for dd in range(2)], [], kst.r(), "kst%d" % s)
            bt, br = bank()
            b2 = bt[:].bitcast(BF16)
            for g in range(4):
                TRN(b2[:, g * 128:(g + 1) * 128], kst.ap[:, g, :], identb[:], kst.r() + ["identb"], [br])
            cp("act", sKT.ap[:, s, :, 0:128], b2[:, 0:512].rearrange("p (g k) -> p g k", g=4), [br], sKT.r(s))
            outs.extend(["owks%d" % s, "owvs%d" % s])
            DMA("sp", [(owks_d[s, 0:120, :], cwk[s, 8:128, :])], [], ["owks%d" % s], "owks%d" % s)
            DMA("sp", [(owvs_d[s, 0:120, :], cwv[s, 8:128, :])], [], ["owvs%d" % s], "owvs%d" % s)

    def ffn(l, TT_=None, mid=None):
        TL = TT_ or T0
        if mid is not None:
            item(mid)
        for ti_, (a, b) in enumerate(TL):
            norm_tile(lambda c, a=a, b=b: xT.ap[:, c, a:b], lambda c, a=a, b=b: xT.r(c, (a, b)), b - a, 2 + l,
                      lambda c, a=a, b=b: mixT.ap[:, c, a:b], lambda c, a=a, b=b: mixT.r(c, (a, b)), rrv=rr3[ti_])
        sgn = [0]
        for g in range(11):
            ring = actr[g % 2]
            for j in range(4):
                f = 4 * g + j

                def comp(s, j=j, ring=ring):
                    Ws = s_in[s]
                    for (a, b) in TL:
                        n = b - a
                        btg, brg = bank()
                        mm16(btg[:, :n], brg, lambda kc: Ws.ap[:, kc, 0:128], lambda kc: mixT.ap[:, kc, a:b], lambda kc: Ws.r() + mixT.r(kc, (a, b)))
                        btu, bru = bank()
                        mm16(btu[:, :n], bru, lambda kc: Ws.ap[:, kc, 128:256], lambda kc: mixT.ap[:, kc, a:b], lambda kc: Ws.r() + mixT.r(kc, (a, b)))
                        sgb = sg[sgn[0] % 2]
                        sgn[0] += 1
                        ACTF(sgb.ap[:, :n], btg[:, :n], AF.Silu, [brg], sgb.r())
                        TT("dve", ring.ap[:, j, a:b], sgb.ap[:, :n], btu[:, :n], ALU.mult, sgb.r() + [bru], ring.r(j, (a, b)))
                item(comp, win2(wg[l], f * 128, wu[l], f * 128))
            for half in range(2):
                def comp(s, half=half, ring=ring):
                    Ws = s_r4[s]
                    for dd in range(8):
                        d = 8 * half + dd
                        for (a, b) in TL:
                            n = b - a
                            bt, br = bank()
                            for rc in range(4):
                                MM(bt[:, :n], Ws.ap[:, rc, dd * 128:(dd + 1) * 128], ring.ap[:, rc, a:b], rc == 0, rc == 3,
                                   Ws.r() + ring.r(rc, (a, b)), [br])
                            TT("dve", xT.ap[:, d, a:b], xT.ap[:, d, a:b], bt[:, :n], ALU.add, xT.r(d, (a, b)) + [br], xT.r(d, (a, b)))
                item(comp, wrow4(wd[l], 4 * g, 1024 * half))

    def layer0():
        pa = collect(phase_a)
        mk = collect(lambda: (mem_kv(0, mkT, mv), mem_kv(1, mkT1p, mv1p)))
        interleave(mk, pa)

        def comp_state(_):
            X = xs[0]
            DMA("sp", [(X.ap[:8, 0:1536], sconv[:, :])], [], X.r(), "xs0")
            for q in range(3):
                bt, br = bank()
                for j in range(4):
                    c = 4 * q + j
                    TRN(bt[:, j * 128:j * 128 + 8], X.ap[:8, c * 128:(c + 1) * 128], ident[:8, :8], X.r() + ["ident"], [br])
                cp("act", sT[:, 4 * q:4 * q + 4, :], bt[:].rearrange("p (a b) -> p a b", a=4)[:, :, :8], [br], ["sT"])
        item(comp_state)
        cn = [0]

        def conv_tile(ti, a, b):
            n = b - a
            npr = n if ti < 2 else C_SAMP - a
            norm_tile(lambda c, a=a, b=b: xT.ap[:, c, a:b], lambda c, a=a, b=b: xT.r(c, (a, b)), n, 0,
                      lambda c, n=n: hTt.ap[:, c, :n], lambda c: hTt.r(c))
            for j2 in range(6):
                ys_l = []
                for jj in range(2):
                    ci = 2 * j2 + jj

                    def comp(s, ci=ci, a=a, b=b, n=n, npr=npr, ti=ti, ys_l=ys_l):
                        Ws = s_in[s]
                        k = cn[0] % 2
                        cn[0] += 1
                        C, CU, Y = csb[k], cub[k], yb[k]
                        ys_l.append((k, ci))
                        cr = ["carry%d" % ci]
                        bt, br = bank()
                        mm16(bt[:, :n], br, lambda kc: Ws.ap[:, kc, 0:128], lambda kc: hTt.ap[:, kc, :n], lambda kc: Ws.r() + hTt.r(kc))
                        cp("act", C.ap[:, :n], bt[:, :n], [br], C.r())
                        bt2, br2 = bank()
                        mm16(bt2[:, :n], br2, lambda kc: Ws.ap[:, kc, 128:256], lambda kc: hTt.ap[:, kc, :n], lambda kc: Ws.r() + hTt.r(kc))
                        cp("pool", CU.ap[:, 0:2], carry[:, ci, :], cr, CU.r())
                        TT("dve", CU.ap[:, 2:2 + n], C.ap[:, :n], bt2[:, :n], ALU.mult, C.r() + [br2], CU.r())
                        cp("pool", carry[:, ci, :], CU.ap[:, npr:npr + 2], CU.r(), cr)
                        ACTF(Y.ap[:, :npr], CU.ap[:, 0:npr], AF.Copy, CU.r() + ["convw"], Y.r(), scale=convw[:, ci, 0:1])
                        for kk in (1, 2):
                            STT(Y.ap[:, :npr], CU.ap[:, kk:kk + npr], convw[:, ci, kk:kk + 1], Y.ap[:, :npr], ALU.mult, ALU.add,
                                CU.r() + Y.r() + ["convw"], Y.r())
                        if ti == 2:
                            CS, YS = cus[k], ysb[k]
                            cp("pool", CS.ap[:, :, 0:2], sT[:, ci, :].rearrange("p (s k) -> p s k", s=4), ["sT"], CS.r())
                            cp("pool", CS.ap[:, :, 2:10], CU.ap[:, 2 + npr:2 + n].rearrange("p (s k) -> p s k", s=4), CU.r(), CS.r())
                            ACTF(YS.ap, CS.ap[:, :, 0:8], AF.Copy, CS.r() + ["convw"], YS.r(), scale=convw[:, ci, 0:1])
                            for kk in (1, 2):
                                STT(YS.ap, CS.ap[:, :, kk:kk + 8], convw[:, ci, kk:kk + 1], YS.ap, ALU.mult, ALU.add, CS.r() + YS.r() + ["convw"], YS.r())
                            cp("pool", Y.ap[:, npr:n].rearrange("p (s k) -> p s k", s=4), YS.ap, YS.r(), Y.r())
                            cp("pool", oconvT[:, ci, 0:2], CU.ap[:, npr:npr + 2], CU.r(), ["oconvT"])
                            cp("pool", oconvT[:, ci, 2:10].rearrange("p (s k) -> p s k", s=4), CS.ap[:, :, 8:10], CS.r(), ["oconvT"])
                    item(comp, win2(cwin, 1536 + ci * 128, cwin, 3072 + ci * 128))

                def compb(s, a=a, b=b, n=n, ys_l=ys_l):
                    Ws = s_in[s]
                    for jj in range(2):
                        k, ci = ys_l[jj]
                        bt, br = bank()
                        mm16(bt[:, :n], br, lambda kc: Ws.ap[:, kc, jj * 128:(jj + 1) * 128], lambda kc: hTt.ap[:, kc, :n], lambda kc: Ws.r() + hTt.r(kc))
                        TT("dve", mixT.ap[:, ci, a:b], yb[k].ap[:, :n], bt[:, :n], ALU.mult, yb[k].r() + [br], mixT.r(ci, (a, b)))
                item(compb, win(cwin, j2 * 256))
            for hb in range(2):
                def compq(s, hb=hb, a=a, b=b, n=n):
                    Ws = s_in[s]
                    for hh in range(2):
                        h = 2 * hb + hh
                        bt, br = bank()
                        mm16(bt[:, :n], br, lambda kc: Ws.ap[:, kc, hh * 128:(hh + 1) * 128], lambda kc: hTt.ap[:, kc, :n], lambda kc: Ws.r() + hTt.r(kc))
                        ACTF(mixT.ap[:, 12 + h, a:b], bt[:, :n], AF.Copy, [br], mixT.r(12 + h, (a, b)), scale=128 ** -0.5)
                item(compq, win(cwin, 4608 + hb * 256))
        ctiles = [collect(lambda ti=ti, a=a, b=b: conv_tile(ti, a, b)) for ti, (a, b) in enumerate(T0)]
        xsteps = attn_steps(xattn(0, [(C_HALO + 128 * i, 128) for i in range(9)]))
        items.extend(ctiles[0])
        set_pools([5, 6, 7], [0, 1, 2], [3, 4], [3, 4])
        start_feed(xsteps[0:12], 50)
        items.extend(ctiles[1])
        flush_feed()
        start_feed(xsteps[12:28], 25)
        items.extend(ctiles[2])
        flush_feed()
        for c in xsteps[28:]:
            item(c)
        items.append((None, BARRIER))
        set_pools(list(range(8)), list(range(8)), list(range(8)), list(range(8)))

        def comp_oc(_):
            O = ocstg
            for q in range(3):
                bt, br = bank()
                for j in range(4):
                    c = 4 * q + j
                    TRN(bt[:10, j * 128:(j + 1) * 128], oconvT[:, c, :], ident[:], ["oconvT", "ident"], [br])
                cp("act", O.ap[:10, q * 512:(q + 1) * 512], bt[:10, :], [br], O.r())
            outs.append("oconv")
            DMA("sp", [(oconv_d[:, :], O.ap[:10, 0:1536])], O.r(), ["oconv"], "ocstg")
        item(comp_oc)

        def unpark(_):
            for h in range(4):
                cp("act", mkT.ap[:, h, :], mkT1p.ap[:, h, :], mkT1p.r(h), mkT.r(h))
            for mt in range(2):
                cp("act", mv.ap[:, mt, :], mv1p.ap[:, mt, :], mv1p.r(mt), mv.r(mt))
        item(unpark)
        out_proj(cwout)
        ffn(0, None, cache_prep)

    def layer1():

        def comp_cache(_):
            P.op("dve", lambda e: e.memset(mixT.ap[:, :, 0:2], 0.0), w=mixT.r(None, (0, 2)))
        item(comp_cache)

        def proj_tile(ti, a, b):
            n = b - a
            norm_tile(lambda c, a=a, b=b: xT.ap[:, c, a:b], lambda c, a=a, b=b: xT.r(c, (a, b)), n, 1,
                      lambda c, n=n: hTt.ap[:, c, :n], lambda c: hTt.r(c))
            nblk = 4 if ti < 2 else 1
            blk0 = 4 * ti
            kvbanks = []

            def compk(s):
                Ws = s_in[s]
                bt, br = bank()
                mm16(bt[:, 0:256], br, lambda kc: hTt.ap[:, kc, 0:128], lambda kc: Ws.ap[:, kc, :], lambda kc: Ws.r() + hTt.r(kc))
                cp("act", kvstg.ap[:, 0:256], bt[:, 0:256], [br], kvstg.r((0, 256)))
                outs.append("owkp")
                DMA("sp", [(owkvp_d[:, 0:256], kvstg.ap[:, 0:256])], kvstg.r((0, 256)), ["owkp"], "kvstgk")
                for s4 in range(4):
                    bt, br = bank()
                    mm16(bt[:8, 0:256], br, lambda kc: hTt.ap[:, kc, 128 + 8 * s4:136 + 8 * s4], lambda kc: Ws.ap[:, kc, :], lambda kc: Ws.r() + hTt.r(kc))
                    cp("act", kvstg.ap[:8, 0:256], bt[:8, 0:256], [br], kvstg.r((0, 256)))
                    DMA("sp", [(owks_d[s4, 120:128, :], kvstg.ap[:8, 0:256])], kvstg.r((0, 256)), ["owks%d" % s4], "kvstgk")

            def compv(s, nblk=nblk, ti=ti, blk0=blk0):
                Ws = s_in[s]
                for bi in range(nblk):
                    bt, br = bank()
                    mm16(bt[:, 0:256], br, lambda kc: hTt.ap[:, kc, bi * 128:(bi + 1) * 128], lambda kc: Ws.ap[:, kc, :], lambda kc: Ws.r() + hTt.r(kc))
                    vb = blk0 + bi
                    vd5 = Vd.ap[:, vb, :].rearrange("p (g d c) -> p g d c", g=4, d=2)
                    src = bt[:, 0:256].rearrange("p (g c) -> p g c", g=4)
                    cp("act", vd5[:, :, 0, :], src, [br], Vd.r(vb))
                    cp("dve", vd5[:, :, 1, :], src, [br], Vd.r(vb))
                    if ti == 2:
                        cp("act", kvstg.ap[:, 256:512], bt[:, 0:256], [br], kvstg.r((256, 512)))
                        outs.append("owvp")
                        DMA("sp", [(owkvp_d[:, 256:512], kvstg.ap[:, 256:512])], kvstg.r((256, 512)), ["owvp"], "kvstgv")
                if ti == 2:
                    for s4 in range(4):
                        bt, br = bank()
                        mm16(bt[:8, 0:256], br, lambda kc: hTt.ap[:, kc, 128 + 8 * s4:136 + 8 * s4], lambda kc: Ws.ap[:, kc, :], lambda kc: Ws.r() + hTt.r(kc))
                        vn5 = sVn.ap[:8, s4, :].rearrange("p (g d c) -> p g d c", g=4, d=2)
                        src = bt[:8, 0:256].rearrange("p (g c) -> p g c", g=4)
                        cp("act", vn5[:, :, 0, :], src, [br], sVn.r(s4))
                        cp("dve", vn5[:, :, 1, :], src, [br], sVn.r(s4))
                        cp("act", kvstg.ap[:8, 256:512], bt[:8, 0:256], [br], kvstg.r((256, 512)))
                        DMA("sp", [(owvs_d[s4, 120:128, :], kvstg.ap[:8, 256:512])], kvstg.r((256, 512)), ["owvs%d" % s4], "kvstgv")
            item(compv, win(awin, 1792))
            def compkt(s, a=a, n=n, ti=ti):
                Ws = s_in[s]
                npr = n if ti < 2 else 128
                for gp in range(2):
                    bt, br = bank()
                    mm16(bt[:, :n], br, lambda kc: Ws.ap[:, kc, gp * 128:(gp + 1) * 128], lambda kc: hTt.ap[:, kc, :n], lambda kc: Ws.r() + hTt.r(kc))
                    for half in range(2):
                        g = 2 * gp + half
                        p0, p1 = 64 * half, 64 * half + 64
                        cp("act" if half else "dve", KT.ap[p0:p1, g, a - 2:a - 2 + npr], bt[p0:p1, :npr], [br], KT.r(g, (a - 2, a - 2 + npr)))
                        if ti == 2:
                            cp("act", sKT.ap[p0:p1, :, g, 128:136], bt[p0:p1, 128:160].rearrange("p (s k) -> p s k", s=4), [br], sKT.r())
                for g in range(4):
                    half = g % 2
                    p0, p1 = 64 * half, 64 * half + 64
                    q0, q1 = 64 * (1 - half), 64 * (1 - half) + 64
                    bt, br = bank()
                    MM(bt[:, :npr], swp[p0:p1, :], KT.ap[p0:p1, g, a - 2:a - 2 + npr], True, True, KT.r(g, (a - 2, a - 2 + npr)) + ["swp"], [br])
                    cp("dve" if half else "act", KT.ap[q0:q1, g, a - 2:a - 2 + npr], bt[q0:q1, :npr], [br], KT.r(g, (a - 2, a - 2 + npr)))
                    if ti == 2:
                        bt2, br2 = bank()
                        MM(bt2[:, 0:32], swp[p0:p1, :], sKT.ap[p0:p1, :, g, 128:136], True, True, sKT.r() + ["swp"], [br2])
                        cp("act", sKT.ap[q0:q1, :, g, 128:136], bt2[q0:q1, 0:32].rearrange("p (s k) -> p s k", s=4), [br2], sKT.r())
                if ti == 2:
                    compk(s)
            item(compkt, win(awin, 1536))
            for cb in range(8):
                def compq(s, cb=cb, a=a, b=b, n=n):
                    Ws = s_in[s]
                    for jj in range(2):
                        c = (2 * cb + jj) if cb < 6 else (12 + 2 * (cb - 6) + jj)
                        bt, br = bank()
                        mm16(bt[:, :n], br, lambda kc: Ws.ap[:, kc, jj * 128:(jj + 1) * 128], lambda kc: hTt.ap[:, kc, :n], lambda kc: Ws.r() + hTt.r(kc))
                        qs = 0.125 if cb < 6 else 128 ** -0.5
                        if jj:
                            ACTF(mixT.ap[:, c, a:b], bt[:, :n], AF.Copy, [br], mixT.r(c, (a, b)), scale=qs)
                        else:
                            TS("dve", mixT.ap[:, c, a:b], bt[:, :n], qs, ALU.mult, [br], mixT.r(c, (a, b)))
                item(compq, win(awin, cb * 256 if cb < 6 else 2048 + (cb - 6) * 256))
        tiles = [collect(lambda ti=ti, a=a, b=b: proj_tile(ti, a, b)) for ti, (a, b) in enumerate(T1)]
        def swa_block(i):
            us = []
            c0 = C_MAIN + 128 * i
            for g in range(4):
                for hh in (0, 2, 4, 1, 3, 5):
                    h = 6 * g + hh
                    c, par = h // 2, h % 2
                    p0, p1 = par * 64, par * 64 + 64
                    vch = [(Vd.ap[:, i + j, g * 128:(g + 1) * 128], Vd.r(i + j), j * 128, 128) for j in range(2)]
                    us.append(attn_unit(128, mixT.ap[p0:p1, c, c0:c0 + 128], mixT.r(c, (c0, c0 + 128)), KT.ap[p0:p1, g, i * 128:(i + 2) * 128],
                                        KT.r(g, (i * 128, (i + 2) * 128)), 256, maskb[:, 1 if i == 0 else 0, :], h, vch,
                                        mixT.ap[p0:p1, c, c0:c0 + 128], mixT.r(c, (c0, c0 + 128)), p0, p1))
            return us

        xb = lambda lo, hi: xattn(1, [(C_MAIN + 128 * i, 128) for i in range(lo, hi)], False)
        units = []
        for i in range(3):
            units += swa_block(i)
        units += xb(0, 3)
        for i in range(3, 7):
            units += swa_block(i)
        units += xb(3, 7)
        units += swa_block(7)
        units += xb(7, 8)
        for s in range(4):
            c0 = C_SAMP + 8 * s
            for g in range(4):
                for par in range(2):
                    p0, p1 = par * 64, par * 64 + 64
                    vch = [(sVc.ap[:, s, g * 128:(g + 1) * 128], sVc.r(s), 0, 128), (sVn.ap[:8, s, g * 128:(g + 1) * 128], sVn.r(s), 128, 8)]
                    qa = mixT.ap[p0:p1, 3 * g:3 * g + 3, c0:c0 + 8]
                    qr = mixT.r((3 * g, 3 * g + 3), (c0, c0 + 8))
                    units.append(attn_unit(24, Qs.ap[p0:p1, 4 * s + g, :], Qs.r(4 * s + g), sKT.ap[p0:p1, s, g, :], sKT.r(s, g), 136,
                                           maskb[:24, 2, 0:136], 2 * g + par, vch,
                                           qa, qr, p0, p1, snk_t=snk24, out_src=lambda a: a.rearrange("p (j q) -> p j q", j=3)))
        steps = attn_steps(units)
        items.extend(tiles[0])
        set_pools([5, 6, 7], [0, 1, 2], [3, 4], [3, 4])
        start_feed(steps[0:84], 4)
        items.extend(tiles[1])
        flush_feed()
        start_feed(steps[84:196], 3)
        items.extend(tiles[2])
        flush_feed()
        def gatherq(_):
            for s4 in range(4):
                for g in range(4):
                    c0 = C_SAMP + 8 * s4
                    cp("pool", Qs.ap[:, 4 * s4 + g, :].rearrange("p (j q) -> p j q", j=3), mixT.ap[:, 3 * g:3 * g + 3, c0:c0 + 8],
                       mixT.r((3 * g, 3 * g + 3), (c0, c0 + 8)), Qs.r(4 * s4 + g))
        item(gatherq)
        for c in steps[196:]:
            item(c)
        set_pools([5, 6, 7], [0, 1, 2], [3, 4], [3, 4])
        attn_run(xattn(1, [], True))
        items.append((None, BARRIER))
        set_pools(list(range(8)), list(range(8)), list(range(8)), list(range(8)))
        out_proj(awout, T1m)
        ffn(1, T1m)

    layer0()
    layer1()

    yTb = [View(RA, 8192 * i, (16, 128), F32) for i in range(2)]
    ostgB = [View(RB, 8192 * i, (D,), F32) for i in range(2)]
    on = [0]
    for ti_, (a, b) in enumerate(T1m):
        n = b - a

        def stats(_, a=a, b=b, n=n, rr=rrf[ti_]):
            bt, br = bank()
            for c in range(16):
                q = xsq[c % 2]
                ACTF(q.ap[:, :n], xT.ap[:, c, a:b], AF.Square, xT.r(c, (a, b)), q.r())
                MM(bt[:, :n], onesb[:], q.ap[:, :n], c == 0, c == 15, q.r() + ["onesb"], [br])
            ACTF(rr.ap[:, :n], bt[:, :n], AF.Sqrt, [br], rr.r(), scale=1.0 / D, bias=EPS)
            P.op("dve", lambda e: e.reciprocal(out=rr.ap[:, :n], in_=rr.ap[:, :n]), r=rr.r(), w=rr.r())
        item(stats)
    for ti_, (a, b) in enumerate(T1m):
        n = b - a
        for bo in range(0, n, 128):
            nb = min(128, n - bo)

            def compf(_, a=a, bo=bo, nb=nb, rr=rrf[ti_]):
                k = on[0] % 2
                on[0] += 1
                O, Y = ostgB[k], yTb[k]
                for c in range(16):
                    STT(Y.ap[:, c, :nb], xT.ap[:, c, a + bo:a + bo + nb], gvec[:, 4, c:c + 1], rr.ap[:, bo:bo + nb], ALU.mult, ALU.mult,
                        xT.r(c, (a + bo, a + bo + nb)) + rr.r() + ["gvec"], Y.r(c))
                for q in range(4):
                    bt, br = bank()
                    for j in range(4):
                        c = 4 * q + j
                        TRN(bt[:nb, j * 128:(j + 1) * 128], Y.ap[:, c, :nb], ident[:], Y.r(c) + ["ident"], [br])
                    cp("act", O.ap[:nb, q * 512:(q + 1) * 512], bt[:nb, :], [br], O.r())
                r0 = a - C_MAIN + bo
                outs.append("y%d" % r0)
                DMA("sp", [(y_d[r0:r0 + nb, :], O.ap[:nb, :])], O.r(), ["y%d" % r0], "ostgB%d" % k)
            item(compf)

    kstop = int(os.environ.get('KSTOP', '0'))
    if kstop > 0:
        del items[kstop:]
    lds = [i for i, (ld, _) in enumerate(items) if ld is not None]
    bars = [i for i, (ld, comp) in enumerate(items) if comp is BARRIER]
    nl = 0
    li = 0
    slot_of = {}
    for i, (ld, comp) in enumerate(items):
        lim = min([b for b in bars if b > i] + [len(items)])
        while nl < len(lds) and nl < li + NS and lds[nl] < lim:
            s = nl % NS
            items[lds[nl]][0](s)
            slot_of[lds[nl]] = s
            nl += 1
        if comp is not BARRIER:
            comp(slot_of.get(i))
        if ld is not None:
            li += 1
    P.op("sp", None, r=outs)
    import collections
    print("ops per engine:", dict(collections.Counter(o["eng"] for o in P.ops)), "items", len(items), "loads", len(lds))
    P.emit()
    return nc


def _chunked(v):
    return np.ascontiguousarray(np.asarray(v, np.float32).reshape(16, 128).T)


def make_in_maps(inp):
    x_prompt = np.asarray(inp["x_prompt"], np.float32)
    x_sample = np.asarray(inp["x_sample"], np.float32)
    gv = np.stack([_chunked(inp["norm_mix"][0]), _chunked(inp["norm_mix"][1]), _chunked(inp["norm_ffn"][0]), _chunked(inp["norm_ffn"][1]),
                   _chunked(inp["norm_final"]), _chunked(inp["norm_mem"][0]), _chunked(inp["norm_mem"][1])], axis=1).reshape(128, 7 * 16)
    cw = np.ascontiguousarray(np.asarray(inp["conv_w"], np.float32)[0].reshape(3, 12, 128).transpose(2, 1, 0)).reshape(128, 36)
    sinks = np.ascontiguousarray(np.broadcast_to(np.asarray(inp["attn_sinks"], np.float32)[0][None, :], (128, 24)))
    qi = np.arange(128)[:, None]
    kj = np.arange(256)[None, :]
    band = (kj > qi) & (kj <= qi + 128)
    m_std = np.where(band, 0.0, NEG).astype(np.float32)
    m_first = np.where(band & (kj >= 128), 0.0, NEG).astype(np.float32)
    m_s24 = np.full((128, 256), NEG, np.float32)
    m_s24[0:24] = np.tile(m_std[0:8], (3, 1))
    sk = np.asarray(inp["attn_sinks"], np.float32)[0]
    s24 = np.zeros((128, 8), np.float32)
    for g_ in range(4):
        for par_ in range(2):
            for j_ in range(3):
                s24[8 * j_:8 * j_ + 8, 2 * g_ + par_] = sk[6 * g_ + 2 * j_ + par_]
    shared = {k: np.ascontiguousarray(np.asarray(inp[k], np.float32)) for k in ("w_mem_kv", "w_gate", "w_up", "w_down")}
    shared["sinks24"] = s24
    shared["conv_w_in"] = np.ascontiguousarray(np.asarray(inp["conv_w_in"], np.float32)[0])
    shared["conv_w_out"] = np.ascontiguousarray(np.asarray(inp["conv_w_out"], np.float32)[0])
    shared["attn_w_in"] = np.ascontiguousarray(np.asarray(inp["attn_w_in"], np.float32)[0])
    shared["attn_w_out"] = np.ascontiguousarray(np.asarray(inp["attn_w_out"], np.float32)[0])
    shared["gvec"] = np.ascontiguousarray(gv)
    shared["convw"] = cw
    shared["sinks"] = sinks
    shared["ident"] = np.eye(128, dtype=np.float32)
    maps = []
    for c in range(8):
        b, hf = c // 2, c % 2
        xe = np.zeros((TOK, D), np.float32)
        if hf == 1:
            xe[0:130] = x_prompt[b, 1024 - 130:1024]
        xe[130:1154] = x_prompt[b, hf * 1024:(hf + 1) * 1024]
        xe[1154:] = x_sample[4 * c:4 * c + 4].reshape(32, D)
        m = dict(shared)
        m["xe"] = xe
        m["mem"] = np.ascontiguousarray(np.asarray(inp["mem_prompt"], np.float32)[b])
        m["sconv"] = np.ascontiguousarray(np.asarray(inp["state_conv"], np.float32)[0, 4 * c:4 * c + 4].reshape(8, 1536))
        m["cwk"] = np.ascontiguousarray(np.asarray(inp["cache_win_k"], np.float32)[0, 4 * c:4 * c + 4].reshape(4, 128, 256))
        m["cwv"] = np.ascontiguousarray(np.asarray(inp["cache_win_v"], np.float32)[0, 4 * c:4 * c + 4].reshape(4, 128, 256))
        m["cmk"] = np.ascontiguousarray(np.asarray(inp["cache_mem_k"], np.float32)[:, 4 * c:4 * c + 4].reshape(2, 4, 256, 512))
        m["cmv"] = np.ascontiguousarray(np.asarray(inp["cache_mem_v"], np.float32)[:, 4 * c:4 * c + 4].reshape(2, 4, 256, 512))
        m["mask"] = np.ascontiguousarray(np.stack([m_std, m_first if hf == 0 else m_std, m_s24], axis=1).reshape(128, 768))
        maps.append(m)
    return maps


def assemble(results):
    y_prompt = np.zeros((4, 2048, D), np.float32)
    y_sample = np.zeros((32, 8, D), np.float32)
    ncp = np.zeros((1, 4, 2, 1536), np.float32)
    ncs = np.zeros((1, 32, 2, 1536), np.float32)
    wkp = np.zeros((1, 4, 128, 4, 64), np.float32)
    wvp = np.zeros((1, 4, 128, 4, 64), np.float32)
    wks = np.zeros((1, 32, 128, 4, 64), np.float32)
    wvs = np.zeros((1, 32, 128, 4, 64), np.float32)
    mkp = np.zeros((2, 4, 256, 4, 128), np.float32)
    mvp = np.zeros((2, 4, 256, 4, 128), np.float32)
    for c in range(8):
        r = results[c]
        b, hf = c // 2, c % 2
        y_prompt[b, hf * 1024:(hf + 1) * 1024] = r["y"][0:1024]
        y_sample[4 * c:4 * c + 4] = r["y"][1024:1056].reshape(4, 8, D)
        ncs[0, 4 * c:4 * c + 4] = r["o_conv"][2:10].reshape(4, 2, 1536)
        wks[0, 4 * c:4 * c + 4] = r["o_wk_s"].reshape(4, 128, 4, 64)
        wvs[0, 4 * c:4 * c + 4] = r["o_wv_s"].reshape(4, 128, 4, 64)
        if hf == 1:
            ncp[0, b] = r["o_conv"][0:2]
            wkp[0, b] = r["o_wkv_p"][:, 0:256].reshape(128, 4, 64)
            wvp[0, b] = r["o_wkv_p"][:, 256:512].reshape(128, 4, 64)
        else:
            for l in range(2):
                mkp[l, b] = r["o_mkv"][l][:, 0:512].reshape(256, 4, 128)
                mvp[l, b] = r["o_mkv"][l][:, 512:1024].reshape(256, 4, 128)
    return (y_prompt, y_sample, ncp, ncs, wkp, wvp, wks, wvs, mkp, mvp)


def kernel(**inputs):
    maps = make_in_maps(inputs)
    nc = build()
    res = run_bass_kernel_spmd(nc, maps, core_ids=list(range(8)))
    return assemble(res.results)
```

```python
import os
import numpy as np
from contextlib import ExitStack
import concourse.bass as bass
import concourse.mybir as mybir
from concourse.bass_utils import run_bass_kernel_spmd

F32 = mybir.dt.float32
BF16 = mybir.dt.bfloat16
AF = mybir.ActivationFunctionType
ALU = mybir.AluOpType
AX = mybir.AxisListType
ENGS = ("pe", "act", "dve", "pool", "sp")

D = 2048
FF = 5632
TOK = 1186
C_HALO, C_MAIN, C_SAMP = 2, 130, 1154
NS = 4
EPS = 1e-6
NEG = -30000.0


class Prog:
    def __init__(self, nc):
        self.nc = nc
        self.ops = []
        self.last_w = {}
        self.readers = {}
        self.es = ExitStack()
        self.dma_keys = []

    def sb(self, name, shape, dt):
        return self.es.enter_context(self.nc.sbuf_tensor("sb_" + name, list(shape), dt))

    def ps(self, name, shape, dt=F32):
        return self.es.enter_context(self.nc.psum_tensor(name, list(shape), dt))

    def op(self, eng, fn, r=(), w=(), dma=None, ndma=1):
        w = list(w) + [x for x in r if x.startswith("ps")]
        r = [x for x in r if not x.startswith("ps")]
        i = len(self.ops)
        deps = set()
        for x in r:
            j = self.last_w.get(x)
            if j is not None:
                deps.add(j)
        for x in w:
            j = self.last_w.get(x)
            if j is not None:
                deps.add(j)
            deps.update(self.readers.get(x, ()))
        for x in r:
            self.readers.setdefault(x, []).append(i)
        for x in w:
            self.last_w[x] = i
            self.readers[x] = []
        if dma is not None and dma not in self.dma_keys:
            self.dma_keys.append(dma)
        self.ops.append(dict(eng=eng, fn=fn, deps=deps, dma=dma, ndma=ndma, sig=False))
        return i

    def emit(self):
        nc = self.nc
        ops = self.ops
        for o in ops:
            nd = set()
            for j in o["deps"]:
                p = ops[j]
                if p["dma"] is None and o["dma"] is None and p["eng"] == "pe" and o["eng"] == "pe":
                    continue
                nd.add(j)
                p["sig"] = True
            o["deps"] = nd
        cnt = {e: 0 for e in ENGS}
        dcnt = {k: 0 for k in self.dma_keys}
        for o in ops:
            if o["dma"] is not None:
                dcnt[o["dma"]] += 16 * o["ndma"]
                o["sv"] = dcnt[o["dma"]]
            elif o["sig"]:
                cnt[o["eng"]] += 1
                o["sv"] = cnt[o["eng"]]
        esem = {e: self.es.enter_context(nc.semaphore("s_" + e)) for e in ENGS}
        dsem = {k: self.es.enter_context(nc.semaphore("d_%d" % n)) for n, k in enumerate(self.dma_keys)}

        def run(eng_name, eh):
            waited = {}
            for o in ops:
                if o["eng"] != eng_name:
                    continue
                need = {}
                for j in o["deps"]:
                    p = ops[j]
                    key = ("d", p["dma"]) if p["dma"] is not None else ("e", p["eng"])
                    if p["sv"] > need.get(key, 0):
                        need[key] = p["sv"]
                for key, v in need.items():
                    if waited.get(key, 0) >= v:
                        continue
                    waited[key] = v
                    sem = dsem[key[1]] if key[0] == "d" else esem[key[1]]
                    eh.wait_ge(sem, v)
                if o["fn"] is None:
                    continue
                ins = o["fn"](eh)
                if o["dma"] is not None:
                    if not isinstance(ins, (list, tuple)):
                        ins = [ins]
                    assert len(ins) == o["ndma"], (len(ins), o["ndma"])
                    for x in ins:
                        x.then_inc(dsem[o["dma"]], 16)
                elif o["sig"]:
                    ins.then_inc(esem[eng_name], 1)

        with nc.Block() as block:
            @block.tensor
            def _(e):
                run("pe", e)

            @block.scalar
            def _(e):
                run("act", e)

            @block.vector
            def _(e):
                run("dve", e)

            @block.gpsimd
            def _(e):
                run("pool", e)

            @block.sync
            def _(e):
                run("sp", e)
        self.es.close()


class Reg:
    def __init__(self, P, name, nbytes, gran=512):
        self.name = name
        self.gran = gran
        self.nbytes = nbytes
        self.t = P.sb(name, [128, nbytes // 2], BF16)


class View:
    def __init__(self, reg, off, shape, dt):
        self.reg, self.off, self.shape, self.dt = reg, off, tuple(shape), dt
        self.esz = 2 if dt == BF16 else 4
        n = int(np.prod(shape))
        assert off % 4 == 0 and off + n * self.esz <= reg.nbytes, (reg.name, off, shape)
        raw = reg.t[:, off // 2:(off + n * self.esz) // 2]
        if dt == F32:
            raw = raw.bitcast(F32)
        if len(shape) == 2:
            raw = raw.rearrange("p (a b) -> p a b", a=shape[0])
        elif len(shape) == 3:
            raw = raw.rearrange("p (a b c) -> p a b c", a=shape[0], b=shape[1])
        elif len(shape) == 4:
            raw = raw.rearrange("p (a b c d) -> p a b c d", a=shape[0], b=shape[1], c=shape[2])
        self.ap = raw
        self.strides = [int(np.prod(shape[i + 1:])) * self.esz for i in range(len(shape))]

    def r(self, *idx):
        idx = list(idx) + [None] * (len(self.shape) - len(idx))
        rngs = []
        for d, ix in enumerate(idx):
            if ix is None:
                rngs.append((0, self.shape[d]))
            elif isinstance(ix, tuple):
                rngs.append(ix)
            else:
                rngs.append((ix, ix + 1))
        names = set()
        g = self.reg.gran

        def rec(d, base):
            a, b = rngs[d]
            full_after = all(rngs[k] == (0, self.shape[k]) for k in range(d + 1, len(rngs)))
            if full_after or d == len(rngs) - 1:
                lo = base + a * self.strides[d]
                hi = base + b * self.strides[d]
                for i in range(lo // g, (hi - 1) // g + 1):
                    names.add("%s%d" % (self.reg.name, i))
            else:
                for i in range(a, b):
                    rec(d + 1, base + i * self.strides[d])

        rec(0, self.off)
        return sorted(names)


def build():
    nc = bass.Bass("TRN2", target_bir_lowering=False)

    def din(name, shape, dt=F32):
        return nc.dram_tensor(name, list(shape), dt, kind="ExternalInput").ap()

    def dout(name, shape):
        return nc.dram_tensor(name, list(shape), F32, kind="ExternalOutput").ap()

    xe = din("xe", [TOK, D])
    mem = din("mem", [256, D])
    sconv = din("sconv", [8, 1536])
    cwk = din("cwk", [4, 128, 256])
    cwv = din("cwv", [4, 128, 256])
    cmk = din("cmk", [2, 4, 256, 512])
    cmv = din("cmv", [2, 4, 256, 512])
    wmem = din("w_mem_kv", [2, D, 1024])
    wg = din("w_gate", [2, D, FF])
    wu = din("w_up", [2, D, FF])
    wd = din("w_down", [2, FF, D])
    cwin = din("conv_w_in", [D, 5120])
    cwout = din("conv_w_out", [D, D])
    awin = din("attn_w_in", [D, 2560])
    awout = din("attn_w_out", [D, D])
    gvec_d = din("gvec", [128, 7 * 16])
    convw_d = din("convw", [128, 36])
    sinks_d = din("sinks", [128, 24])
    ident_d = din("ident", [128, 128])
    mask_d = din("mask", [128, 768])
    sinks24_d = din("sinks24", [128, 8])
    y_d = dout("y", [1056, D])
    oconv_d = dout("o_conv", [10, 1536])
    owkvp_d = dout("o_wkv_p", [128, 512])
    owks_d = dout("o_wk_s", [4, 128, 256])
    owvs_d = dout("o_wv_s", [4, 128, 256])
    omkv_d = dout("o_mkv", [2, 256, 1024])

    P = Prog(nc)
    RX = Reg(P, "rx", 16 * TOK * 4, gran=1024)
    xT = View(RX, 0, (16, TOK), F32)
    RA = Reg(P, "ra", 16 * TOK * 2)
    mixT = View(RA, 0, (16, TOK), BF16)
    xs = [View(RA, 8192 * i, (D,), F32) for i in range(2)]
    memT = View(RA, 16384, (16, 256), F32)
    stg = [View(RA, 32768 + 1024 * i, (256,), F32) for i in range(2)]
    yT = View(RA, 0, (16, 128), F32)
    mstg = View(RA, 34816, (3, 256), F32)
    ostg = [View(RA, 8192 + 8192 * i, (D,), F32) for i in range(2)]
    RB = Reg(P, "rb", 20480)
    hTt = View(RB, 0, (16, 512), BF16)
    hmemT = View(RB, 0, (16, 256), BF16)
    xsq = [View(RB, 16384 + 1024 * i, (512,), BF16) for i in range(2)]
    rr = View(RB, 18432, (512,), F32)
    rr_default = rr
    rr3 = [View(RB, 2048 * i, (512,), F32) for i in range(3)]
    rrf = [View(RA, 16384 + 2048 * i, (512,), F32) for i in range(3)]
    smk = [View(RB, 2048 * i, (4, 256), BF16) for i in range(4)]
    smv = [View(RB, 8192 + 2048 * i, (2, 512), BF16) for i in range(4)]
    cmkst = View(RB, 16384, (2, 512), BF16)
    actr = [View(RB, 9488 * i, (4, TOK), BF16) for i in range(2)]
    Qs = View(RB, 0, (16, 24), BF16)
    ocstg = View(RB, 0, (1536,), F32)
    RC = Reg(P, "rc", 37120)
    csb = [View(RC, 2048 * i, (512,), F32) for i in range(2)]
    cub = [View(RC, 4096 + 2064 * i, (516,), F32) for i in range(2)]
    yb = [View(RC, 8224 + 2048 * i, (512,), F32) for i in range(2)]
    cus = [View(RC, 12320 + 160 * i, (4, 10), F32) for i in range(2)]
    ysb = [View(RC, 12640 + 128 * i, (4, 8), F32) for i in range(2)]
    KT = View(RC, 0, (4, 1152), BF16)
    ksts = [View(RC, 13568 + 1024 * i, (4, 128), BF16) for i in range(4)]
    sKT = View(RC, 9216, (4, 4, 136), BF16)
    Vd = View(RC, 13568, (9, 512), BF16)
    sVc = View(RC, 22784, (4, 512), BF16)
    sVn = View(RC, 26880, (4, 512), BF16)
    kvstg = View(RC, 30976, (512,), F32)
    mkT = View(RC, 33024, (4, 256), BF16)
    mv = View(RC, 35072, (2, 512), BF16)
    mkT1p = View(RC, 16384, (4, 256), BF16)
    mv1p = View(RC, 18432, (2, 512), BF16)
    sg = [View(RC, 2048 * i, (512,), F32) for i in range(2)]
    RS = [Reg(P, "slot%d" % i, 8192, gran=8192) for i in range(NS)]
    s_in = [View(RS[i], 0, (16, 256), BF16) for i in range(NS)]
    s_row = [View(RS[i], 0, (2, D), BF16) for i in range(NS)]
    s_kd = [View(RS[i], 0, (16, 2, 2, 64), BF16) for i in range(NS)]
    s_r4 = [View(RS[i], 0, (4, 1024), BF16) for i in range(NS)]
    Er = [View(Reg(P, "er%d" % i, 528, gran=528), 0, (264,), BF16) for i in range(2)]
    Pr = [View(Reg(P, "pr%d" % i, 512, gran=512), 0, (256,), BF16) for i in range(2)]
    PTr = [View(Reg(P, "ptr%d" % i, 512, gran=512), 0, (2, 128), BF16) for i in range(2)]
    ident = P.sb("ident", [128, 128], F32)
    identb = P.sb("identb", [128, 128], BF16)
    onesb = P.sb("onesb", [128, 128], BF16)
    maskb = P.sb("maskb", [128, 3, 256], BF16)
    onesf = P.sb("onesf", [1, 128], F32)
    swp = P.sb("swp", [128, 128], BF16)
    snk = P.sb("snk", [128, 2, 24], BF16)
    snk24 = P.sb("snk24", [128, 2, 8], BF16)
    s24f = P.sb("s24f", [128, 2, 8], F32)
    snkf = P.sb("snkf", [128, 24], F32)
    gvec = P.sb("gvec", [128, 7, 16], F32)
    convw = P.sb("convw", [128, 12, 3], F32)
    sinks = P.sb("sinks", [128, 24], F32)
    carry = P.sb("carry", [128, 12, 2], F32)
    oconvT = P.sb("oconvT", [128, 12, 10], F32)
    sT = P.sb("sT", [128, 12, 8], F32)
    st = [P.sb("st%d" % i, [128, 8], F32) for i in range(6)]
    banks = [P.ps("bank%d" % i, [128, 512], F32) for i in range(8)]
    POOLS = {"G": list(range(8)), "S": list(range(8)), "PT": list(range(8)), "O": list(range(8)), "KV": [3, 4, 5, 6, 7]}
    bk = {"G": 0, "S": 0, "PT": 0, "O": 0, "KV": 0}

    def set_pools(g, s_, pt, o):
        def comp(_):
            POOLS["G"], POOLS["S"], POOLS["PT"], POOLS["O"] = g, s_, pt, o
        item(comp)

    def bank(pool="G"):
        if pool != "KV" and POOLS[pool] == POOLS["G"]:
            pool = "G"
        lst = POOLS[pool]
        i = lst[bk[pool] % len(lst)]
        bk[pool] += 1
        return banks[i], "ps%d" % i

    items = []
    outs = []
    BARRIER = object()

    def item(comp, ld=None):
        items.append((ld, comp))

    def MM(out, lhsT, rhs, start, stop, r, w):
        P.op("pe", lambda e: e.matmul(out, lhsT=lhsT, rhs=rhs, start=start, stop=stop), r=r, w=w)

    def TRN(out, in_, idn, r, w):
        P.op("pe", lambda e: e.transpose(out=out, in_=in_, identity=idn), r=r, w=w)

    def ACTF(out, in_, func, r, w, **kw):
        P.op("act", lambda e: e.activation(out=out, in_=in_, func=func, **kw), r=r, w=w)

    def TT(eng, out, in0, in1, op, r, w):
        P.op(eng, lambda e: e.tensor_tensor(out=out, in0=in0, in1=in1, op=op), r=r, w=w)

    def TS(eng, out, in0, s1, op0, r, w):
        P.op(eng, lambda e: e.tensor_scalar(out=out, in0=in0, scalar1=s1, scalar2=None, op0=op0), r=r, w=w)

    def STT(out, in0, scalar, in1, op0, op1, r, w):
        P.op("dve", lambda e: e.scalar_tensor_tensor(out=out, in0=in0, scalar=scalar, in1=in1, op0=op0, op1=op1), r=r, w=w)

    pq = [0]
    PQD = int(os.environ.get("PQD", "8"))

    def DMA(eng, pairs, r, w, key):
        if eng == "pool":
            w = list(w) + ["pq%d" % (pq[0] % PQD)]
            pq[0] += 1
        P.op(eng, lambda e: [e.dma_start(out=o, in_=i) for o, i in pairs], r=r, w=w, dma=key, ndma=len(pairs))

    def cp(eng, out, in_, r, w):
        if eng == "act":
            ACTF(out, in_, AF.Copy, r, w)
        else:
            P.op(eng, lambda e: e.tensor_copy(out=out, in_=in_), r=r, w=w)

    feed = {"steps": [], "every": 0, "cnt": 0}

    def tick():
        if feed["every"] and feed["steps"]:
            feed["cnt"] += 1
            if feed["cnt"] % feed["every"] == 0:
                feed["steps"].pop(0)(None)

    def start_feed(steps, every):
        def comp(_):
            feed["steps"], feed["every"], feed["cnt"] = list(steps), every, 0
        item(comp)

    def flush_feed():
        def comp(_):
            while feed["steps"]:
                feed["steps"].pop(0)(None)
            feed["every"] = 0
        item(comp)

    def mm16(out_ap, br, lhs, rhs, rres):
        for kc in range(16):
            MM(out_ap, lhs(kc), rhs(kc), kc == 0, kc == 15, rres(kc), [br])
            tick()

    def consts(_):
        for name, t, src in (("ident", ident[:], ident_d[:, :]),
                             ("gvec", gvec[:], gvec_d.rearrange("p (a b) -> p a b", a=7)),
                             ("convw", convw[:], convw_d.rearrange("p (a b) -> p a b", a=12)), ("sinks", sinks[:], sinks_d[:, :])):
            DMA("sp", [(t, src)], [], [name], name)
        ACTF(identb[:], ident[:], AF.Copy, ["ident"], ["identb"])
        DMA("sp", [(mstg.ap, mask_d.rearrange("p (a b) -> p a b", a=3))], [], mstg.r(), "mstg")
        ACTF(maskb[:], mstg.ap, AF.Copy, mstg.r(), ["maskb"])
        P.op("dve", lambda e: e.memset(onesf[:], 1.0), w=["onesf"])
        P.op("dve", lambda e: e.memset(swp[:], 0.0), w=["swp"])
        cp("dve", snk[:, 0, :], sinks[:], ["sinks"], ["snk"])
        TT("dve", snkf[:], sinks[:], snk[:, 0, :], ALU.subtract, ["sinks", "snk"], ["snkf"])
        cp("dve", snk[:, 1, :], snkf[:], ["snkf"], ["snk"])
        DMA("sp", [(s24f[:, 0, :], sinks24_d[:, :])], [], ["s24f"], "s24f")
        cp("dve", snk24[:, 0, :], s24f[:, 0, :], ["s24f"], ["snk24"])
        TT("dve", s24f[:, 1, :], s24f[:, 0, :], snk24[:, 0, :], ALU.subtract, ["s24f", "snk24"], ["s24f"])
        cp("dve", snk24[:, 1, :], s24f[:, 1, :], ["s24f"], ["snk24"])
        cp("dve", swp[0:64, 64:128], identb[0:64, 0:64], ["identb"], ["swp"])
        cp("dve", swp[64:128, 0:64], identb[64:128, 64:128], ["identb"], ["swp"])
        P.op("dve", lambda e: e.memset(onesb[:], 1.0), w=["onesb"])
        P.op("dve", lambda e: e.memset(carry[:], 0.0), w=["carry%d" % i for i in range(12)])
        P.op("dve", lambda e: e.memset(oconvT[:], 0.0), w=["oconvT"])

    item(consts)

    ldT_n = [0]

    def load_T(rows, n, dst4, dst4_res):
        def comp(_):
            b = ldT_n[0] % 2
            ldT_n[0] += 1
            X = xs[b]
            DMA("sp", [(X.ap[:n, :], rows)], [], X.r(), "xs%d" % b)
            for q in range(4):
                bt, br = bank()
                for j in range(4):
                    c = 4 * q + j
                    TRN(bt[:, j * 128:j * 128 + n], X.ap[:n, c * 128:(c + 1) * 128], ident[:n, :n], X.r() + ["ident"], [br])
                src = bt[:].rearrange("p (a b) -> p a b", a=4)[:, :, :n]
                cp("act" if q % 2 == 0 else "dve", dst4(q), src, [br], dst4_res(q))
        item(comp)

    def norm_tile(src, src_res, n, gi, dst, dst_res, rrv=None):
        def comp(_, rr=None):
            rr = rrv if rrv is not None else rr_default
            bt, br = bank()
            for c in range(16):
                q = xsq[c % 2]
                ACTF(q.ap[:, :n], src(c), AF.Square, src_res(c), q.r())
                MM(bt[:, :n], onesb[:], q.ap[:, :n], c == 0, c == 15, q.r() + ["onesb"], [br])
            ACTF(rr.ap[:, :n], bt[:, :n], AF.Sqrt, [br], rr.r(), scale=1.0 / D, bias=EPS)
            P.op("dve", lambda e: e.reciprocal(out=rr.ap[:, :n], in_=rr.ap[:, :n]), r=rr.r(), w=rr.r())
            for c in range(16):
                STT(dst(c), src(c), gvec[:, gi, c:c + 1], rr.ap[:, :n], ALU.mult, ALU.mult, src_res(c) + rr.r() + ["gvec"], dst_res(c))
        item(comp)

    def win(W, c0, ncols=256):
        src = W.rearrange("(kc p) n -> p kc n", p=128)[:, :, c0:c0 + ncols]

        def ld(s):
            DMA("pool", [(s_in[s].ap[:, :, 0:ncols], src)], [], s_in[s].r(), "slot%d" % s)
        return ld

    def win2(Wa, ca, Wb, cb):
        sa = Wa.rearrange("(kc p) n -> p kc n", p=128)[:, :, ca:ca + 128]
        sb_ = Wb.rearrange("(kc p) n -> p kc n", p=128)[:, :, cb:cb + 128]

        def ld(s):
            DMA("pool", [(s_in[s].ap[:, :, 0:128], sa), (s_in[s].ap[:, :, 128:256], sb_)], [], s_in[s].r(), "slot%d" % s)
        return ld

    def wrow4(W, r0, c0):
        src = W.rearrange("(rc p) n -> p rc n", p=128)[:, r0:r0 + 4, c0:c0 + 1024]

        def ld(s):
            DMA("pool", [(s_r4[s].ap, src)], [], s_r4[s].r(), "slot%d" % s)
        return ld

    au = [0]
    NST = 8

    def attn_unit(nq, q_ap, q_res, kt_ap, kt_res, nk, mask_ap, sink_ap, vch, out_ap, out_res, p0, p1, xsets=False, pre=None, snk_t=None, out_src=None):
        d = {}
        ne = nk + 1 if mask_ap is not None else nk

        def s0():
            if pre is not None:
                pre()
            u = au[0]
            au[0] += 1
            d["u"] = u
            d.update(E=Er[u % 2], Pm=Pr[u % 2], PT=PTr[u % 2])
            d.update(stt=st[u % 6], sres=["st%d" % (u % 6)])
            bt, br = bank("S")
            if mask_ap is not None:
                MM(bt[:nq, :nk], q_ap, kt_ap, True, False, q_res + kt_res, [br])
                MM(bt[:nq, :nk], identb[:nq, :nq], mask_ap, False, True, ["identb", "maskb"], [br])
                sk = snk if snk_t is None else snk_t
                MM(bt[:nq, nk:ne], identb[:nq, :nq], sk[:nq, 0, sink_ap:sink_ap + 1], True, False, ["identb", "snk", "snk24"], [br])
                MM(bt[:nq, nk:ne], identb[:nq, :nq], sk[:nq, 1, sink_ap:sink_ap + 1], False, True, ["identb", "snk", "snk24"], [br])
            else:
                MM(bt[:nq, :nk], q_ap, kt_ap, True, True, q_res + kt_res, [br])
            d.update(bt=bt, br=br)

        def s1():
            stt, sres, bt, br = d["stt"], d["sres"], d["bt"], d["br"]
            sin, sres_in = bt[:nq, :ne], [br]
            P.op("dve", lambda e: e.tensor_reduce(out=stt[:nq, 1:2], in_=sin, axis=AX.X, op=ALU.max, negate=True), r=sres_in, w=sres)
            d.update(sin=sin, sres_in=sres_in)

        def s2():
            E, stt, sres, sin, sres_in = d["E"], d["stt"], d["sres"], d["sin"], d["sres_in"]
            ACTF(E.ap[:nq, :ne], sin, AF.Exp, sres_in + sres, E.r() + sres, bias=stt[:nq, 1:2], scale=1.0, accum_out=stt[:nq, 2:3])

        def s3():
            E, Pm, stt, sres = d["E"], d["Pm"], d["stt"], d["sres"]
            P.op("dve", lambda e: e.reciprocal(out=stt[:nq, 5:6], in_=stt[:nq, 2:3]), r=sres, w=sres)
            TS("dve", Pm.ap[:nq, :nk], E.ap[:nq, :nk], stt[:nq, 5:6], ALU.mult, E.r() + sres, Pm.r())

        def s4():
            Pm = d["Pm"]
            lst = POOLS["PT"]
            bi_ = lst[d["u"] % len(lst)]
            bt2, br2 = banks[bi_], "ps%d" % bi_
            b2 = bt2[:].bitcast(BF16)
            for j, (v_ap, v_res, koff, nkc) in enumerate(vch):
                TRN(b2[:nkc, j * 128:j * 128 + nq], Pm.ap[:nq, koff:koff + nkc], identb[:nq, :nq], Pm.r() + ["identb"], [br2])
            d.update(b2=b2, br2=br2)

        def s5():
            PT, b2, br2 = d["PT"], d["b2"], d["br2"]
            eng = "act"
            if all(v[3] == 128 for v in vch):
                cp(eng, PT.ap[:, :, :nq], b2[:, 0:256].rearrange("p (j q) -> p j q", j=2)[:, :, :nq], [br2], PT.r())
            else:
                for j, (v_ap, v_res, koff, nkc) in enumerate(vch):
                    cp(eng, PT.ap[:nkc, j, :nq], b2[:nkc, j * 128:j * 128 + nq], [br2], PT.r())

        def s6():
            PT = d["PT"]
            lst = POOLS["PT"]
            bi_ = lst[d["u"] % len(lst)]
            bt3, br3 = banks[bi_], "ps%d" % bi_
            for j, (v_ap, v_res, koff, nkc) in enumerate(vch):
                MM(bt3[:, 128:128 + nq], v_ap, PT.ap[:nkc, j, :nq], j == 0, j == len(vch) - 1, PT.r() + v_res, [br3])
            d.update(bt3=bt3, br3=br3)

        def s7():
            src = d["bt3"][p0:p1, 128:128 + nq]
            cp("dve", out_ap, src if out_src is None else out_src(src), [d["br3"]], out_res)
        return [s0, s1, s2, s3, s4, s5, s6, s7]

    def attn_steps(units):
        n = len(units)
        steps = []
        for t in range(n + NST - 1):
            def comp(_, t=t):
                for sidx in range(NST - 1, -1, -1):
                    k = t - sidx
                    if 0 <= k < n:
                        units[k][sidx]()
            steps.append(comp)
        return steps

    def attn_run(units):
        for c in attn_steps(units):
            item(c)

    def collect(fn):
        n0 = len(items)
        fn()
        out = items[n0:]
        del items[n0:]
        return out

    def interleave(A, steps):
        na, nb = len(A), len(steps)
        j = 0
        for i, it in enumerate(A):
            items.append(it)
            tgt = (i + 1) * nb // na
            while j < tgt:
                items.append(steps[j]) if isinstance(steps[j], tuple) else item(steps[j])
                j += 1
        while j < nb:
            items.append(steps[j]) if isinstance(steps[j], tuple) else item(steps[j])
            j += 1

    T0 = [(0, 512), (512, 1024), (1024, TOK)]
    T1 = [(2, 514), (514, 1026), (1026, TOK)]
    T1m = [(130, 642), (642, 1154), (1154, TOK)]

    def phase_a():
        for rt in range(10):
            r0 = rt * 128
            n = min(128, TOK - r0)
            load_T(xe[r0:r0 + n, :], n, lambda q, r0=r0, n=n: xT.ap[:, 4 * q:4 * q + 4, r0:r0 + n],
                   lambda q, r0=r0, n=n: xT.r((4 * q, 4 * q + 4), (r0, r0 + n)))

    def mem_kv(l, mkT, mv):
        for mt in range(2):
            load_T(mem[mt * 128:(mt + 1) * 128, :], 128, lambda q, mt=mt: memT.ap[:, 4 * q:4 * q + 4, mt * 128:(mt + 1) * 128],
                   lambda q, mt=mt: memT.r((4 * q, 4 * q + 4), (mt * 128, (mt + 1) * 128)))
        norm_tile(lambda c: memT.ap[:, c, :], lambda c: memT.r(c), 256, 5 + l, lambda c: hmemT.ap[:, c, :], lambda c: hmemT.r(c))
        sn = [0]
        for blk in range(4):
            def comp(s, blk=blk):
                W = s_in[s]
                if blk < 2:
                    for hh in range(2):
                        h = 2 * blk + hh
                        bt, br = bank()
                        mm16(bt[:, 0:256], br, lambda kc: W.ap[:, kc, hh * 128:(hh + 1) * 128], lambda kc: hmemT.ap[:, kc, :],
                             lambda kc: W.r() + hmemT.r(kc))
                        cp("act", mkT.ap[:, h, :], bt[:, 0:256], [br], mkT.r(h))
                for mt in range(2):
                    bt, br = bank()
                    mm16(bt[:, 0:256], br, lambda kc: hmemT.ap[:, kc, mt * 128:(mt + 1) * 128], lambda kc: W.ap[:, kc, :],
                         lambda kc: W.r() + hmemT.r(kc))
                    k = sn[0] % 2
                    sn[0] += 1
                    sb_ = stg[k]
                    cp("dve", sb_.ap, bt[:, 0:256], [br], sb_.r())
                    if blk >= 2:
                        cp("act", mv.ap[:, mt, (blk - 2) * 256:(blk - 1) * 256], bt[:, 0:256], [br], mv.r(mt))
                    on_ = "omkv%d_%d_%d" % (l, blk, mt)
                    outs.append(on_)
                    DMA("sp", [(omkv_d[l, mt * 128:(mt + 1) * 128, blk * 256:(blk + 1) * 256], sb_.ap)], sb_.r(), [on_], "stg%d" % k)
            item(comp, win(wmem[l], blk * 256))

    def xattn(l, blocks, with_sample=True):
        units = []
        for (c0, nq) in blocks:
            for h in range(4):
                vch = [(mv.ap[:, mt, h * 128:(h + 1) * 128], mv.r(mt), mt * 128, 128) for mt in range(2)]
                units.append(attn_unit(nq, mixT.ap[:, 12 + h, c0:c0 + nq], mixT.r(12 + h, (c0, c0 + nq)), mkT.ap[:, h, :], mkT.r(h), 256, None, None, vch,
                                       mixT.ap[:, 12 + h, c0:c0 + nq], mixT.r(12 + h, (c0, c0 + nq)), 0, 128, xsets=True))
        for s in (range(4) if with_sample else []):
            b = s

            def prep(s=s, b=b):
                DMA("pool", [(cmkst.ap, cmk[l, s].rearrange("(mt p) f -> p mt f", p=128))], [], cmkst.r(), "cmkst")
                DMA("pool", [(smv[b].ap, cmv[l, s].rearrange("(mt p) f -> p mt f", p=128))], [], smv[b].r(), "smv%d" % b)
                for h in range(4):
                    bt, br = bank()
                    b2 = bt[:].bitcast(BF16)
                    for mt in range(2):
                        TRN(b2[:, mt * 128:(mt + 1) * 128], cmkst.ap[:, mt, h * 128:(h + 1) * 128], identb[:], cmkst.r() + ["identb"], [br])
                    cp("dve" if h % 2 else "act", smk[b].ap[:, h, :], b2[:, 0:256], [br], smk[b].r(h))
            c0 = C_SAMP + 8 * s
            for h in range(4):
                vch = [(smv[b].ap[:, mt, h * 128:(h + 1) * 128], smv[b].r(mt), mt * 128, 128) for mt in range(2)]
                units.append(attn_unit(8, mixT.ap[:, 12 + h, c0:c0 + 8], mixT.r(12 + h, (c0, c0 + 8)), smk[b].ap[:, h, :], smk[b].r(h), 256, None, None, vch,
                                       mixT.ap[:, 12 + h, c0:c0 + 8], mixT.r(12 + h, (c0, c0 + 8)), 0, 128, xsets=True, pre=prep if h == 0 else None))
        return units

    def out_proj(W, TT_=None):
        TL = TT_ or T0
        for db in range(8):
            def comp(s, db=db):
                Ws = s_in[s]
                for dd in range(2):
                    d = 2 * db + dd
                    for (a, b) in TL:
                        bt, br = bank()
                        mm16(bt[:, :b - a], br, lambda kc: Ws.ap[:, kc, dd * 128:(dd + 1) * 128], lambda kc: mixT.ap[:, kc, a:b],
                             lambda kc: Ws.r() + mixT.r(kc, (a, b)))
                        TT("dve", xT.ap[:, d, a:b], xT.ap[:, d, a:b], bt[:, :b - a], ALU.add, xT.r(d, (a, b)) + [br], xT.r(d, (a, b)))
            item(comp, win(W, db * 256))

    def cache_prep(_):
        sv5 = sVc.ap.rearrange("p s (g d c) -> p s g d c", g=4, d=2)
        for dd in range(2):
            DMA("pool", [(sv5[:, s, :, dd, :], cwv[s].rearrange("p (g c) -> p g c", g=4)) for s in range(4)], [], sVc.r(), "sVc%d" % dd)
        for s in range(4):
            kst = ksts[s]
            ks4 = kst.ap.rearrange("p g (d c) -> p g d c", d=2)
            DMA("pool", [(ks4[:, :, dd, :], cwk[s].rearrange("p (g c) -> p g c", g=4)) for dd in range(2)], [], kst.r(), "kst%d" % s)
            bt, br = bank()
            b2 = bt[:].bitcast(BF16)
            for g in range(4):
                TRN(b2[:, g * 128:(g + 1) * 128], kst.ap[:, g, :], identb[:], kst.r() + ["identb"], [br])
            cp("act", sKT.ap[:, s, :, 0:128], b2[:, 0:512].rearrange("p (g k) -> p g k", g=4), [br], sKT.r(s))
            outs.extend(["owks%d" % s, "owvs%d" % s])
            DMA("sp", [(owks_d[s, 0:120, :], cwk[s, 8:128, :])], [], ["owks%d" % s], "owks%d" % s)
            DMA("sp", [(owvs_d[s, 0:120, :], cwv[s, 8:128, :])], [], ["owvs%d" % s], "owvs%d" % s)

    def ffn(l, TT_=None, mid=None):
        TL = TT_ or T0
        if mid is not None:
            item(mid)
        for ti_, (a, b) in enumerate(TL):
            norm_tile(lambda c, a=a, b=b: xT.ap[:, c, a:b], lambda c, a=a, b=b: xT.r(c, (a, b)), b - a, 2 + l,
                      lambda c, a=a, b=b: mixT.ap[:, c, a:b], lambda c, a=a, b=b: mixT.r(c, (a, b)), rrv=rr3[ti_])
        sgn = [0]
        for g in range(11):
            ring = actr[g % 2]
            for j in range(4):
                f = 4 * g + j

                def comp(s, j=j, ring=ring):
                    Ws = s_in[s]
                    for (a, b) in TL:
                        n = b - a
                        btg, brg = bank()
                        mm16(btg[:, :n], brg, lambda kc: Ws.ap[:, kc, 0:128], lambda kc: mixT.ap[:, kc, a:b], lambda kc: Ws.r() + mixT.r(kc, (a, b)))
                        btu, bru = bank()
                        mm16(btu[:, :n], bru, lambda kc: Ws.ap[:, kc, 128:256], lambda kc: mixT.ap[:, kc, a:b], lambda kc: Ws.r() + mixT.r(kc, (a, b)))
                        sgb = sg[sgn[0] % 2]
                        sgn[0] += 1
                        ACTF(sgb.ap[:, :n], btg[:, :n], AF.Silu, [brg], sgb.r())
                        TT("dve", ring.ap[:, j, a:b], sgb.ap[:, :n], btu[:, :n], ALU.mult, sgb.r() + [bru], ring.r(j, (a, b)))
                item(comp, win2(wg[l], f * 128, wu[l], f * 128))
            for half in range(2):
                def comp(s, half=half, ring=ring):
                    Ws = s_r4[s]
                    for dd in range(8):
                        d = 8 * half + dd
                        for (a, b) in TL:
                            n = b - a
                            bt, br = bank()
                            for rc in range(4):
                                MM(bt[:, :n], Ws.ap[:, rc, dd * 128:(dd + 1) * 128], ring.ap[:, rc, a:b], rc == 0, rc == 3,
                                   Ws.r() + ring.r(rc, (a, b)), [br])
                            TT("dve", xT.ap[:, d, a:b], xT.ap[:, d, a:b], bt[:, :n], ALU.add, xT.r(d, (a, b)) + [br], xT.r(d, (a, b)))
                item(comp, wrow4(wd[l], 4 * g, 1024 * half))

    def layer0():
        pa = collect(phase_a)
        mk = collect(lambda: (mem_kv(0, mkT, mv), mem_kv(1, mkT1p, mv1p)))
        interleave(mk, pa)

        def comp_state(_):
            X = xs[0]
            DMA("sp", [(X.ap[:8, 0:1536], sconv[:, :])], [], X.r(), "xs0")
            for q in range(3):
                bt, br = bank()
                for j in range(4):
                    c = 4 * q + j
                    TRN(bt[:, j * 128:j * 128 + 8], X.ap[:8, c * 128:(c + 1) * 128], ident[:8, :8], X.r() + ["ident"], [br])
                cp("act", sT[:, 4 * q:4 * q + 4, :], bt[:].rearrange("p (a b) -> p a b", a=4)[:, :, :8], [br], ["sT"])
        item(comp_state)
        cn = [0]

        def conv_tile(ti, a, b):
            n = b - a
            npr = n if ti < 2 else C_SAMP - a
            norm_tile(lambda c, a=a, b=b: xT.ap[:, c, a:b], lambda c, a=a, b=b: xT.r(c, (a, b)), n, 0,
                      lambda c, n=n: hTt.ap[:, c, :n], lambda c: hTt.r(c))
            for j2 in range(6):
                ys_l = []
                for jj in range(2):
                    ci = 2 * j2 + jj

                    def comp(s, ci=ci, a=a, b=b, n=n, npr=npr, ti=ti, ys_l=ys_l):
                        Ws = s_in[s]
                        k = cn[0] % 2
                        cn[0] += 1
                        C, CU, Y = csb[k], cub[k], yb[k]
                        ys_l.append((k, ci))
                        cr = ["carry%d" % ci]
                        bt, br = bank()
                        mm16(bt[:, :n], br, lambda kc: Ws.ap[:, kc, 0:128], lambda kc: hTt.ap[:, kc, :n], lambda kc: Ws.r() + hTt.r(kc))
                        cp("act", C.ap[:, :n], bt[:, :n], [br], C.r())
                        bt2, br2 = bank()
                        mm16(bt2[:, :n], br2, lambda kc: Ws.ap[:, kc, 128:256], lambda kc: hTt.ap[:, kc, :n], lambda kc: Ws.r() + hTt.r(kc))
                        cp("pool", CU.ap[:, 0:2], carry[:, ci, :], cr, CU.r())
                        TT("dve", CU.ap[:, 2:2 + n], C.ap[:, :n], bt2[:, :n], ALU.mult, C.r() + [br2], CU.r())
                        cp("pool", carry[:, ci, :], CU.ap[:, npr:npr + 2], CU.r(), cr)
                        ACTF(Y.ap[:, :npr], CU.ap[:, 0:npr], AF.Copy, CU.r() + ["convw"], Y.r(), scale=convw[:, ci, 0:1])
                        for kk in (1, 2):
                            STT(Y.ap[:, :npr], CU.ap[:, kk:kk + npr], convw[:, ci, kk:kk + 1], Y.ap[:, :npr], ALU.mult, ALU.add,
                                CU.r() + Y.r() + ["convw"], Y.r())
                        if ti == 2:
                            CS, YS = cus[k], ysb[k]
                            cp("pool", CS.ap[:, :, 0:2], sT[:, ci, :].rearrange("p (s k) -> p s k", s=4), ["sT"], CS.r())
                            cp("pool", CS.ap[:, :, 2:10], CU.ap[:, 2 + npr:2 + n].rearrange("p (s k) -> p s k", s=4), CU.r(), CS.r())
                            ACTF(YS.ap, CS.ap[:, :, 0:8], AF.Copy, CS.r() + ["convw"], YS.r(), scale=convw[:, ci, 0:1])
                            for kk in (1, 2):
                                STT(YS.ap, CS.ap[:, :, kk:kk + 8], convw[:, ci, kk:kk + 1], YS.ap, ALU.mult, ALU.add, CS.r() + YS.r() + ["convw"], YS.r())
                            cp("pool", Y.ap[:, npr:n].rearrange("p (s k) -> p s k", s=4), YS.ap, YS.r(), Y.r())
                            cp("pool", oconvT[:, ci, 0:2], CU.ap[:, npr:npr + 2], CU.r(), ["oconvT"])
                            cp("pool", oconvT[:, ci, 2:10].rearrange("p (s k) -> p s k", s=4), CS.ap[:, :, 8:10], CS.r(), ["oconvT"])
                    item(comp, win2(cwin, 1536 + ci * 128, cwin, 3072 + ci * 128))

                def compb(s, a=a, b=b, n=n, ys_l=ys_l):
                    Ws = s_in[s]
                    for jj in range(2):
                        k, ci = ys_l[jj]
                        bt, br = bank()
                        mm16(bt[:, :n], br, lambda kc: Ws.ap[:, kc, jj * 128:(jj + 1) * 128], lambda kc: hTt.ap[:, kc, :n], lambda kc: Ws.r() + hTt.r(kc))
                        TT("dve", mixT.ap[:, ci, a:b], yb[k].ap[:, :n], bt[:, :n], ALU.mult, yb[k].r() + [br], mixT.r(ci, (a, b)))
                item(compb, win(cwin, j2 * 256))
            for hb in range(2):
                def compq(s, hb=hb, a=a, b=b, n=n):
                    Ws = s_in[s]
                    for hh in range(2):
                        h = 2 * hb + hh
                        bt, br = bank()
                        mm16(bt[:, :n], br, lambda kc: Ws.ap[:, kc, hh * 128:(hh + 1) * 128], lambda kc: hTt.ap[:, kc, :n], lambda kc: Ws.r() + hTt.r(kc))
                        ACTF(mixT.ap[:, 12 + h, a:b], bt[:, :n], AF.Copy, [br], mixT.r(12 + h, (a, b)), scale=128 ** -0.5)
                item(compq, win(cwin, 4608 + hb * 256))
        ctiles = [collect(lambda ti=ti, a=a, b=b: conv_tile(ti, a, b)) for ti, (a, b) in enumerate(T0)]
        xsteps = attn_steps(xattn(0, [(C_HALO + 128 * i, 128) for i in range(9)]))
        items.extend(ctiles[0])
        set_pools([5, 6, 7], [0, 1, 2], [3, 4], [3, 4])
        start_feed(xsteps[0:12], 50)
        items.extend(ctiles[1])
        flush_feed()
        start_feed(xsteps[12:28], 25)
        items.extend(ctiles[2])
        flush_feed()
        for c in xsteps[28:49]:
            item(c)
        items.append((None, BARRIER))
        for c in xsteps[49:]:
            item(c)
        set_pools(list(range(8)), list(range(8)), list(range(8)), list(range(8)))

        def comp_oc(_):
            O = ocstg
            for q in range(3):
                bt, br = bank()
                for j in range(4):
                    c = 4 * q + j
                    TRN(bt[:10, j * 128:(j + 1) * 128], oconvT[:, c, :], ident[:], ["oconvT", "ident"], [br])
                cp("act", O.ap[:10, q * 512:(q + 1) * 512], bt[:10, :], [br], O.r())
            outs.append("oconv")
            DMA("sp", [(oconv_d[:, :], O.ap[:10, 0:1536])], O.r(), ["oconv"], "ocstg")
        item(comp_oc)

        def unpark(_):
            for h in range(4):
                cp("act", mkT.ap[:, h, :], mkT1p.ap[:, h, :], mkT1p.r(h), mkT.r(h))
            for mt in range(2):
                cp("act", mv.ap[:, mt, :], mv1p.ap[:, mt, :], mv1p.r(mt), mv.r(mt))
        item(unpark)
        out_proj(cwout)
        ffn(0, None, cache_prep)

    def layer1():

        def comp_cache(_):
            P.op("dve", lambda e: e.memset(mixT.ap[:, :, 0:2], 0.0), w=mixT.r(None, (0, 2)))
        item(comp_cache)

        def proj_tile(ti, a, b):
            n = b - a
            norm_tile(lambda c, a=a, b=b: xT.ap[:, c, a:b], lambda c, a=a, b=b: xT.r(c, (a, b)), n, 1,
                      lambda c, n=n: hTt.ap[:, c, :n], lambda c: hTt.r(c))
            nblk = 4 if ti < 2 else 1
            blk0 = 4 * ti
            kvbanks = []

            def compk(s):
                Ws = s_in[s]
                bt, br = bank()
                mm16(bt[:, 0:256], br, lambda kc: hTt.ap[:, kc, 0:128], lambda kc: Ws.ap[:, kc, :], lambda kc: Ws.r() + hTt.r(kc))
                cp("act", kvstg.ap[:, 0:256], bt[:, 0:256], [br], kvstg.r((0, 256)))
                outs.append("owkp")
                DMA("sp", [(owkvp_d[:, 0:256], kvstg.ap[:, 0:256])], kvstg.r((0, 256)), ["owkp"], "kvstgk")
                for s4 in range(4):
                    bt, br = bank()
                    mm16(bt[:8, 0:256], br, lambda kc: hTt.ap[:, kc, 128 + 8 * s4:136 + 8 * s4], lambda kc: Ws.ap[:, kc, :], lambda kc: Ws.r() + hTt.r(kc))
                    cp("act", kvstg.ap[:8, 0:256], bt[:8, 0:256], [br], kvstg.r((0, 256)))
                    DMA("sp", [(owks_d[s4, 120:128, :], kvstg.ap[:8, 0:256])], kvstg.r((0, 256)), ["owks%d" % s4], "kvstgk")

            def compv(s, nblk=nblk, ti=ti, blk0=blk0):
                Ws = s_in[s]
                for bi in range(nblk):
                    bt, br = bank()
                    mm16(bt[:, 0:256], br, lambda kc: hTt.ap[:, kc, bi * 128:(bi + 1) * 128], lambda kc: Ws.ap[:, kc, :], lambda kc: Ws.r() + hTt.r(kc))
                    vb = blk0 + bi
                    vd5 = Vd.ap[:, vb, :].rearrange("p (g d c) -> p g d c", g=4, d=2)
                    src = bt[:, 0:256].rearrange("p (g c) -> p g c", g=4)
                    cp("act", vd5[:, :, 0, :], src, [br], Vd.r(vb))
                    cp("dve", vd5[:, :, 1, :], src, [br], Vd.r(vb))
                    if ti == 2:
                        cp("act", kvstg.ap[:, 256:512], bt[:, 0:256], [br], kvstg.r((256, 512)))
                        outs.append("owvp")
                        DMA("sp", [(owkvp_d[:, 256:512], kvstg.ap[:, 256:512])], kvstg.r((256, 512)), ["owvp"], "kvstgv")
                if ti == 2:
                    for s4 in range(4):
                        bt, br = bank()
                        mm16(bt[:8, 0:256], br, lambda kc: hTt.ap[:, kc, 128 + 8 * s4:136 + 8 * s4], lambda kc: Ws.ap[:, kc, :], lambda kc: Ws.r() + hTt.r(kc))
                        vn5 = sVn.ap[:8, s4, :].rearrange("p (g d c) -> p g d c", g=4, d=2)
                        src = bt[:8, 0:256].rearrange("p (g c) -> p g c", g=4)
                        cp("act", vn5[:, :, 0, :], src, [br], sVn.r(s4))
                        cp("dve", vn5[:, :, 1, :], src, [br], sVn.r(s4))
                        cp("act", kvstg.ap[:8, 256:512], bt[:8, 0:256], [br], kvstg.r((256, 512)))
                        DMA("sp", [(owvs_d[s4, 120:128, :], kvstg.ap[:8, 256:512])], kvstg.r((256, 512)), ["owvs%d" % s4], "kvstgv")
            item(compv, win(awin, 1792))
            def compkt(s, a=a, n=n, ti=ti):
                Ws = s_in[s]
                npr = n if ti < 2 else 128
                for gp in range(2):
                    bt, br = bank()
                    mm16(bt[:, :n], br, lambda kc: Ws.ap[:, kc, gp * 128:(gp + 1) * 128], lambda kc: hTt.ap[:, kc, :n], lambda kc: Ws.r() + hTt.r(kc))
                    for half in range(2):
                        g = 2 * gp + half
                        p0, p1 = 64 * half, 64 * half + 64
                        cp("act" if half else "dve", KT.ap[p0:p1, g, a - 2:a - 2 + npr], bt[p0:p1, :npr], [br], KT.r(g, (a - 2, a - 2 + npr)))
                        if ti == 2:
                            cp("act", sKT.ap[p0:p1, :, g, 128:136], bt[p0:p1, 128:160].rearrange("p (s k) -> p s k", s=4), [br], sKT.r())
                for g in range(4):
                    half = g % 2
                    p0, p1 = 64 * half, 64 * half + 64
                    q0, q1 = 64 * (1 - half), 64 * (1 - half) + 64
                    bt, br = bank()
                    MM(bt[:, :npr], swp[p0:p1, :], KT.ap[p0:p1, g, a - 2:a - 2 + npr], True, True, KT.r(g, (a - 2, a - 2 + npr)) + ["swp"], [br])
                    cp("dve" if half else "act", KT.ap[q0:q1, g, a - 2:a - 2 + npr], bt[q0:q1, :npr], [br], KT.r(g, (a - 2, a - 2 + npr)))
                    if ti == 2:
                        bt2, br2 = bank()
                        MM(bt2[:, 0:32], swp[p0:p1, :], sKT.ap[p0:p1, :, g, 128:136], True, True, sKT.r() + ["swp"], [br2])
                        cp("act", sKT.ap[q0:q1, :, g, 128:136], bt2[q0:q1, 0:32].rearrange("p (s k) -> p s k", s=4), [br2], sKT.r())
                if ti == 2:
                    compk(s)
            item(compkt, win(awin, 1536))
            for cb in range(8):
                def compq(s, cb=cb, a=a, b=b, n=n):
                    Ws = s_in[s]
                    for jj in range(2):
                        c = (2 * cb + jj) if cb < 6 else (12 + 2 * (cb - 6) + jj)
                        bt, br = bank()
                        mm16(bt[:, :n], br, lambda kc: Ws.ap[:, kc, jj * 128:(jj + 1) * 128], lambda kc: hTt.ap[:, kc, :n], lambda kc: Ws.r() + hTt.r(kc))
                        qs = 0.125 if cb < 6 else 128 ** -0.5
                        if jj:
                            ACTF(mixT.ap[:, c, a:b], bt[:, :n], AF.Copy, [br], mixT.r(c, (a, b)), scale=qs)
                        else:
                            TS("dve", mixT.ap[:, c, a:b], bt[:, :n], qs, ALU.mult, [br], mixT.r(c, (a, b)))
                item(compq, win(awin, cb * 256 if cb < 6 else 2048 + (cb - 6) * 256))
        tiles = [collect(lambda ti=ti, a=a, b=b: proj_tile(ti, a, b)) for ti, (a, b) in enumerate(T1)]
        def swa_block(i):
            us = []
            c0 = C_MAIN + 128 * i
            for g in range(4):
                for hh in (0, 2, 4, 1, 3, 5):
                    h = 6 * g + hh
                    c, par = h // 2, h % 2
                    p0, p1 = par * 64, par * 64 + 64
                    vch = [(Vd.ap[:, i + j, g * 128:(g + 1) * 128], Vd.r(i + j), j * 128, 128) for j in range(2)]
                    us.append(attn_unit(128, mixT.ap[p0:p1, c, c0:c0 + 128], mixT.r(c, (c0, c0 + 128)), KT.ap[p0:p1, g, i * 128:(i + 2) * 128],
                                        KT.r(g, (i * 128, (i + 2) * 128)), 256, maskb[:, 1 if i == 0 else 0, :], h, vch,
                                        mixT.ap[p0:p1, c, c0:c0 + 128], mixT.r(c, (c0, c0 + 128)), p0, p1))
            return us

        xb = lambda lo, hi: xattn(1, [(C_MAIN + 128 * i, 128) for i in range(lo, hi)], False)
        units = []
        for i in range(3):
            units += swa_block(i)
        units += xb(0, 3)
        for i in range(3, 7):
            units += swa_block(i)
        units += xb(3, 7)
        units += swa_block(7)
        units += xb(7, 8)
        for s in range(4):
            c0 = C_SAMP + 8 * s
            for g in range(4):
                for par in range(2):
                    p0, p1 = par * 64, par * 64 + 64
                    vch = [(sVc.ap[:, s, g * 128:(g + 1) * 128], sVc.r(s), 0, 128), (sVn.ap[:8, s, g * 128:(g + 1) * 128], sVn.r(s), 128, 8)]
                    qa = mixT.ap[p0:p1, 3 * g:3 * g + 3, c0:c0 + 8]
                    qr = mixT.r((3 * g, 3 * g + 3), (c0, c0 + 8))
                    units.append(attn_unit(24, Qs.ap[p0:p1, 4 * s + g, :], Qs.r(4 * s + g), sKT.ap[p0:p1, s, g, :], sKT.r(s, g), 136,
                                           maskb[:24, 2, 0:136], 2 * g + par, vch,
                                           qa, qr, p0, p1, snk_t=snk24, out_src=lambda a: a.rearrange("p (j q) -> p j q", j=3)))
        steps = attn_steps(units)
        items.extend(tiles[0])
        set_pools([5, 6, 7], [0, 1, 2], [3, 4], [3, 4])
        start_feed(steps[0:84], 4)
        items.extend(tiles[1])
        flush_feed()
        start_feed(steps[84:196], 3)
        items.extend(tiles[2])
        flush_feed()
        def gatherq(_):
            for s4 in range(4):
                for g in range(4):
                    c0 = C_SAMP + 8 * s4
                    cp("pool", Qs.ap[:, 4 * s4 + g, :].rearrange("p (j q) -> p j q", j=3), mixT.ap[:, 3 * g:3 * g + 3, c0:c0 + 8],
                       mixT.r((3 * g, 3 * g + 3), (c0, c0 + 8)), Qs.r(4 * s4 + g))
        item(gatherq)
        for c in steps[196:]:
            item(c)
        set_pools([5, 6, 7], [0, 1, 2], [3, 4], [3, 4])
        xs_ = attn_steps(xattn(1, [], True))
        for c in xs_[:13]:
            item(c)
        items.append((None, BARRIER))
        for c in xs_[13:]:
            item(c)
        set_pools(list(range(8)), list(range(8)), list(range(8)), list(range(8)))
        out_proj(awout, T1m)
        ffn(1, T1m)

    layer0()
    layer1()

    yTb = [View(RA, 8192 * i, (16, 128), F32) for i in range(2)]
    ostgB = [View(RB, 8192 * i, (D,), F32) for i in range(2)]
    on = [0]
    for ti_, (a, b) in enumerate(T1m):
        n = b - a

        def stats(_, a=a, b=b, n=n, rr=rrf[ti_]):
            bt, br = bank()
            for c in range(16):
                q = xsq[c % 2]
                ACTF(q.ap[:, :n], xT.ap[:, c, a:b], AF.Square, xT.r(c, (a, b)), q.r())
                MM(bt[:, :n], onesb[:], q.ap[:, :n], c == 0, c == 15, q.r() + ["onesb"], [br])
            ACTF(rr.ap[:, :n], bt[:, :n], AF.Sqrt, [br], rr.r(), scale=1.0 / D, bias=EPS)
            P.op("dve", lambda e: e.reciprocal(out=rr.ap[:, :n], in_=rr.ap[:, :n]), r=rr.r(), w=rr.r())
        item(stats)
    for ti_, (a, b) in enumerate(T1m):
        n = b - a
        for bo in range(0, n, 128):
            nb = min(128, n - bo)

            def compf(_, a=a, bo=bo, nb=nb, rr=rrf[ti_]):
                k = on[0] % 2
                on[0] += 1
                O, Y = ostgB[k], yTb[k]
                for c in range(16):
                    STT(Y.ap[:, c, :nb], xT.ap[:, c, a + bo:a + bo + nb], gvec[:, 4, c:c + 1], rr.ap[:, bo:bo + nb], ALU.mult, ALU.mult,
                        xT.r(c, (a + bo, a + bo + nb)) + rr.r() + ["gvec"], Y.r(c))
                for q in range(4):
                    bt, br = bank()
                    for j in range(4):
                        c = 4 * q + j
                        TRN(bt[:nb, j * 128:(j + 1) * 128], Y.ap[:, c, :nb], ident[:], Y.r(c) + ["ident"], [br])
                    cp("act", O.ap[:nb, q * 512:(q + 1) * 512], bt[:nb, :], [br], O.r())
                r0 = a - C_MAIN + bo
                outs.append("y%d" % r0)
                DMA("sp", [(y_d[r0:r0 + nb, :], O.ap[:nb, :])], O.r(), ["y%d" % r0], "ostgB%d" % k)
            item(compf)

    kstop = int(os.environ.get('KSTOP', '0'))
    if kstop > 0:
        del items[kstop:]
    lds = [i for i, (ld, _) in enumerate(items) if ld is not None]
    bars = [i for i, (ld, comp) in enumerate(items) if comp is BARRIER]
    nl = 0
    li = 0
    slot_of = {}
    for i, (ld, comp) in enumerate(items):
        lim = min([b for b in bars if b > i] + [len(items)])
        while nl < len(lds) and nl < li + NS and lds[nl] < lim:
            s = nl % NS
            items[lds[nl]][0](s)
            slot_of[lds[nl]] = s
            nl += 1
        if comp is not BARRIER:
            comp(slot_of.get(i))
        if ld is not None:
            li += 1
    P.op("sp", None, r=outs)
    import collections
    print("ops per engine:", dict(collections.Counter(o["eng"] for o in P.ops)), "items", len(items), "loads", len(lds))
    P.emit()
    return nc


def _chunked(v):
    return np.ascontiguousarray(np.asarray(v, np.float32).reshape(16, 128).T)


def make_in_maps(inp):
    x_prompt = np.asarray(inp["x_prompt"], np.float32)
    x_sample = np.asarray(inp["x_sample"], np.float32)
    gv = np.stack([_chunked(inp["norm_mix"][0]), _chunked(inp["norm_mix"][1]), _chunked(inp["norm_ffn"][0]), _chunked(inp["norm_ffn"][1]),
                   _chunked(inp["norm_final"]), _chunked(inp["norm_mem"][0]), _chunked(inp["norm_mem"][1])], axis=1).reshape(128, 7 * 16)
    cw = np.ascontiguousarray(np.asarray(inp["conv_w"], np.float32)[0].reshape(3, 12, 128).transpose(2, 1, 0)).reshape(128, 36)
    sinks = np.ascontiguousarray(np.broadcast_to(np.asarray(inp["attn_sinks"], np.float32)[0][None, :], (128, 24)))
    qi = np.arange(128)[:, None]
    kj = np.arange(256)[None, :]
    band = (kj > qi) & (kj <= qi + 128)
    m_std = np.where(band, 0.0, NEG).astype(np.float32)
    m_first = np.where(band & (kj >= 128), 0.0, NEG).astype(np.float32)
    m_s24 = np.full((128, 256), NEG, np.float32)
    m_s24[0:24] = np.tile(m_std[0:8], (3, 1))
    sk = np.asarray(inp["attn_sinks"], np.float32)[0]
    s24 = np.zeros((128, 8), np.float32)
    for g_ in range(4):
        for par_ in range(2):
            for j_ in range(3):
                s24[8 * j_:8 * j_ + 8, 2 * g_ + par_] = sk[6 * g_ + 2 * j_ + par_]
    shared = {k: np.ascontiguousarray(np.asarray(inp[k], np.float32)) for k in ("w_mem_kv", "w_gate", "w_up", "w_down")}
    shared["sinks24"] = s24
    shared["conv_w_in"] = np.ascontiguousarray(np.asarray(inp["conv_w_in"], np.float32)[0])
    shared["conv_w_out"] = np.ascontiguousarray(np.asarray(inp["conv_w_out"], np.float32)[0])
    shared["attn_w_in"] = np.ascontiguousarray(np.asarray(inp["attn_w_in"], np.float32)[0])
    shared["attn_w_out"] = np.ascontiguousarray(np.asarray(inp["attn_w_out"], np.float32)[0])
    shared["gvec"] = np.ascontiguousarray(gv)
    shared["convw"] = cw
    shared["sinks"] = sinks
    shared["ident"] = np.eye(128, dtype=np.float32)
    maps = []
    for c in range(8):
        b, hf = c // 2, c % 2
        xe = np.zeros((TOK, D), np.float32)
        if hf == 1:
            xe[0:130] = x_prompt[b, 1024 - 130:1024]
        xe[130:1154] = x_prompt[b, hf * 1024:(hf + 1) * 1024]
        xe[1154:] = x_sample[4 * c:4 * c + 4].reshape(32, D)
        m = dict(shared)
        m["xe"] = xe
        m["mem"] = np.ascontiguousarray(np.asarray(inp["mem_prompt"], np.float32)[b])
        m["sconv"] = np.ascontiguousarray(np.asarray(inp["state_conv"], np.float32)[0, 4 * c:4 * c + 4].reshape(8, 1536))
        m["cwk"] = np.ascontiguousarray(np.asarray(inp["cache_win_k"], np.float32)[0, 4 * c:4 * c + 4].reshape(4, 128, 256))
        m["cwv"] = np.ascontiguousarray(np.asarray(inp["cache_win_v"], np.float32)[0, 4 * c:4 * c + 4].reshape(4, 128, 256))
        m["cmk"] = np.ascontiguousarray(np.asarray(inp["cache_mem_k"], np.float32)[:, 4 * c:4 * c + 4].reshape(2, 4, 256, 512))
        m["cmv"] = np.ascontiguousarray(np.asarray(inp["cache_mem_v"], np.float32)[:, 4 * c:4 * c + 4].reshape(2, 4, 256, 512))
        m["mask"] = np.ascontiguousarray(np.stack([m_std, m_first if hf == 0 else m_std, m_s24], axis=1).reshape(128, 768))
        maps.append(m)
    return maps


def assemble(results):
    y_prompt = np.zeros((4, 2048, D), np.float32)
    y_sample = np.zeros((32, 8, D), np.float32)
    ncp = np.zeros((1, 4, 2, 1536), np.float32)
    ncs = np.zeros((1, 32, 2, 1536), np.float32)
    wkp = np.zeros((1, 4, 128, 4, 64), np.float32)
    wvp = np.zeros((1, 4, 128, 4, 64), np.float32)
    wks = np.zeros((1, 32, 128, 4, 64), np.float32)
    wvs = np.zeros((1, 32, 128, 4, 64), np.float32)
    mkp = np.zeros((2, 4, 256, 4, 128), np.float32)
    mvp = np.zeros((2, 4, 256, 4, 128), np.float32)
    for c in range(8):
        r = results[c]
        b, hf = c // 2, c % 2
        y_prompt[b, hf * 1024:(hf + 1) * 1024] = r["y"][0:1024]
        y_sample[4 * c:4 * c + 4] = r["y"][1024:1056].reshape(4, 8, D)
        ncs[0, 4 * c:4 * c + 4] = r["o_conv"][2:10].reshape(4, 2, 1536)
        wks[0, 4 * c:4 * c + 4] = r["o_wk_s"].reshape(4, 128, 4, 64)
        wvs[0, 4 * c:4 * c + 4] = r["o_wv_s"].reshape(4, 128, 4, 64)
        if hf == 1:
            ncp[0, b] = r["o_conv"][0:2]
            wkp[0, b] = r["o_wkv_p"][:, 0:256].reshape(128, 4, 64)
            wvp[0, b] = r["o_wkv_p"][:, 256:512].reshape(128, 4, 64)
        else:
            for l in range(2):
                mkp[l, b] = r["o_mkv"][l][:, 0:512].reshape(256, 4, 128)
                mvp[l, b] = r["o_mkv"][l][:, 512:1024].reshape(256, 4, 128)
    return (y_prompt, y_sample, ncp, ncs, wkp, wvp, wks, wvs, mkp, mvp)


def kernel(**inputs):
    maps = make_in_maps(inputs)
    nc = build()
    res = run_bass_kernel_spmd(nc, maps, core_ids=list(range(8)))
    return assemble(res.results)
```
